# Optimizing a Trainium2 kernel written in Bass

```python
import math
import jax, jax.numpy as jnp
from jax import lax
import numpy as np

D_MODEL = 1024
BATCH = 4
SEQ = 4096
DEPTH = 4

GRID_W = 64
EPS = 1e-6
NA_HEADS = 8
NA_HEAD_DIM = 64
NA_WIN_R = 8
NA_WIN_C = 16
NA_QCOLS = 16
NA_KCOLS = 2 * NA_WIN_C
GLA_HEADS = 4
GLA_DK = 64
GLA_DV = 128
GLA_GATE_RANK = 16
GLA_GATE_TAU = 16.0
GLA_CHUNK = 64
MLA_HEADS = 4
MLA_Q_RANK = 256
MLA_KV_RANK = 256
MLA_NOPE = 128
MLA_ROPE = 64
MLA_V = 128
MLA_QBLOCK = 128
ROPE_THETA = 10000.0
D_FF = 2816
N_BRANCH = 3
NA_W = NA_HEADS * NA_HEAD_DIM
GLA_QK_W = GLA_HEADS * GLA_DK
GLA_V_W = GLA_HEADS * GLA_DV
MLA_QK_HEAD = MLA_NOPE + MLA_ROPE
MLA_V_W = MLA_HEADS * MLA_V
IN_SPLITS = (NA_W, NA_W, NA_W,
             GLA_QK_W, GLA_QK_W, GLA_V_W, GLA_V_W, GLA_GATE_RANK, GLA_GATE_RANK,
             MLA_Q_RANK, MLA_KV_RANK, MLA_ROPE,
             N_BRANCH * D_MODEL)
D_IN = 6752

kernel_name = "hybrid_na_gla_mla_macaron_encoder"


def rms_norm(x, g):
    xf = x.astype(jnp.float32)
    y = xf * lax.rsqrt(jnp.mean(xf * xf, axis=-1, keepdims=True) + EPS)
    return (y * g.astype(jnp.float32)).astype(x.dtype)


def swiglu(h, w1, w3, w2):
    return (jax.nn.silu(h @ w1) * (h @ w3)) @ w2


def split_cols(z, sizes):
    out, start = [], 0
    for n in sizes:
        out.append(z[..., start:start + n])
        start += n
    return out


def neighborhood_attention(q, k, v, rpb):
    B, S, H, d = q.shape
    rows = S // GRID_W
    win_r = min(NA_WIN_R, rows)
    nj = GRID_W // NA_QCOLS
    r = np.arange(rows)
    r0 = np.clip(r - win_r // 2, 0, rows - win_r)
    ridx = r0[:, None] + np.arange(win_r)
    j = np.arange(nj)
    k0 = np.clip(j * NA_QCOLS - NA_WIN_C // 2, 0, GRID_W - NA_KCOLS)
    cidx = k0[:, None] + np.arange(NA_KCOLS)
    qc = j[:, None] * NA_QCOLS + np.arange(NA_QCOLS)
    c0 = np.clip(qc - NA_WIN_C // 2, 0, GRID_W - NA_WIN_C)
    col_ok = (cidx[:, None, :] >= c0[..., None]) & (cidx[:, None, :] < c0[..., None] + NA_WIN_C)
    mask = np.broadcast_to(col_ok[:, :, None, :], (nj, NA_QCOLS, win_r, NA_KCOLS)).reshape(nj, NA_QCOLS, win_r * NA_KCOLS)
    dr = ridx - r[:, None] + (NA_WIN_R - 1)
    dc = np.clip(cidx[:, None, :] - qc[:, :, None] + (NA_WIN_C - 1), 0, 2 * NA_WIN_C - 2)
    bias = rpb[:, dr[:, None, None, :, None], dc[None, :, :, None, :]]
    bias = bias.reshape(H, rows, nj, NA_QCOLS, win_r * NA_KCOLS)

    ri = ridx[:, None, :, None]
    ci = cidx[None, :, None, :]
    kg = k.reshape(B, rows, GRID_W, H, d)[:, ri, ci].reshape(B, rows, nj, win_r * NA_KCOLS, H, d)
    vg = v.reshape(B, rows, GRID_W, H, d)[:, ri, ci].reshape(B, rows, nj, win_r * NA_KCOLS, H, d)
    qg = q.reshape(B, rows, nj, NA_QCOLS, H, d)
    s = jnp.einsum('brjqhd,brjkhd->bhrjqk', qg, kg).astype(jnp.float32) * (d ** -0.5)
    s = jnp.where(mask, s + bias.astype(jnp.float32), -1e30)
    p = jax.nn.softmax(s, axis=-1).astype(v.dtype)
    o = jnp.einsum('bhrjqk,brjkhd->brjqhd', p, vg)
    return o.reshape(B, S, H * d)


def gla_chunked(q, k, v, g, strict):
    B, H, S, dk = q.shape
    dv = v.shape[-1]
    n = S // GLA_CHUNK
    q = q.reshape(B, H, n, GLA_CHUNK, dk)
    k = k.reshape(B, H, n, GLA_CHUNK, dk)
    g = g.reshape(B, H, n, GLA_CHUNK, dk)
    v = v.reshape(B, H, n, GLA_CHUNK, dv)
    b = jnp.cumsum(g, axis=3)
    b_last = b[:, :, :, -1:, :]
    qe = q * jnp.exp(b)
    ke = k * jnp.exp(-b)
    k_end = k * jnp.exp(b_last - b)
    tri = np.tril(np.ones((GLA_CHUNK, GLA_CHUNK), dtype=bool), -1 if strict else 0)
    a = jnp.where(tri, jnp.einsum('bhnid,bhnjd->bhnij', qe, ke), 0.0)
    o_intra = jnp.einsum('bhnij,bhnjv->bhniv', a, v)
    upd = jnp.einsum('bhncd,bhncv->bhndv', k_end, v)
    decay = jnp.exp(b_last[:, :, :, 0, :])

    def step(state, inp):
        dec, u = inp
        return dec[..., None] * state + u, state

    init = jnp.zeros((B, H, dk, dv), q.dtype)
    _, s_prev = lax.scan(step, init, (jnp.moveaxis(decay, 2, 0), jnp.moveaxis(upd, 2, 0)))
    s_prev = jnp.moveaxis(s_prev, 0, 2)
    o = o_intra + jnp.einsum('bhnid,bhndv->bhniv', qe, s_prev)
    return o.reshape(B, H, S, dv)


def gla_bidirectional(q, k, v, g_fwd, g_bwd):
    t = lambda a: jnp.swapaxes(a.astype(jnp.float32), 1, 2)
    q, k, v, g_fwd, g_bwd = t(q), t(k), t(v), t(g_fwd), t(g_bwd)
    q = q * (GLA_DK ** -0.5)
    flip = lambda a: a[:, :, ::-1]
    o_f = gla_chunked(q, k, v, g_fwd, False)
    o_b = flip(gla_chunked(flip(q), flip(k), flip(v), flip(g_bwd), True))
    return jnp.swapaxes(o_f + o_b, 1, 2)


def rope_tables(S):
    half = MLA_ROPE // 2
    inv = ROPE_THETA ** (-jnp.arange(half, dtype=jnp.float32) / half)
    ang = jnp.arange(S, dtype=jnp.float32)[:, None] * inv[None, :]
    return jnp.cos(ang), jnp.sin(ang)


def apply_rope(x, cos, sin):
    xf = x.astype(jnp.float32)
    x1, x2 = xf[..., :MLA_ROPE // 2], xf[..., MLA_ROPE // 2:]
    c, s = cos[None, :, None, :], sin[None, :, None, :]
    return jnp.concatenate([x1 * c - x2 * s, x1 * s + x2 * c], axis=-1).astype(x.dtype)


def blocked_softmax_attention(q, k, v, scale):
    B, S, H, dq = q.shape
    nb = S // MLA_QBLOCK
    qb = jnp.moveaxis(q.reshape(B, nb, MLA_QBLOCK, H, dq), 1, 0)

    def one(qblk):
        s = jnp.einsum('bqhd,bkhd->bhqk', qblk, k).astype(jnp.float32) * scale
        p = jax.nn.softmax(s, axis=-1).astype(v.dtype)
        return jnp.einsum('bhqk,bkhd->bqhd', p, v)

    o = lax.map(one, qb)
    return jnp.moveaxis(o, 0, 1).reshape(B, S, H * v.shape[-1])


def mla_attention(c_q, c_kv, k_rope, cq_norm, ckv_norm, w_uq, w_ukv, q_norm, k_norm, cos, sin):
    B, S, _ = c_q.shape
    q = (rms_norm(c_q, cq_norm) @ w_uq).reshape(B, S, MLA_HEADS, MLA_QK_HEAD)
    kv = (rms_norm(c_kv, ckv_norm) @ w_ukv).reshape(B, S, MLA_HEADS, MLA_NOPE + MLA_V)
    k_nope, v = kv[..., :MLA_NOPE], kv[..., MLA_NOPE:]
    k_r = jnp.broadcast_to(k_rope[:, :, None, :], (B, S, MLA_HEADS, MLA_ROPE))
    k = jnp.concatenate([k_nope, k_r], axis=-1)
    q = rms_norm(q, q_norm)
    k = rms_norm(k, k_norm)
    q = jnp.concatenate([q[..., :MLA_NOPE], apply_rope(q[..., MLA_NOPE:], cos, sin)], axis=-1)
    k = jnp.concatenate([k[..., :MLA_NOPE], apply_rope(k[..., MLA_NOPE:], cos, sin)], axis=-1)
    return blocked_softmax_attention(q, k, v, MLA_QK_HEAD ** -0.5)


def setup_inputs(seed: int = 0) -> dict:
    key = jax.random.key(seed)
    ks = iter(jax.random.split(key, 32))
    L = DEPTH
    f32 = jnp.float32

    def w(shape, fan_in):
        return jax.random.normal(next(ks), shape, f32) * (fan_in ** -0.5)

    def gain(shape):
        return 1.0 + 0.05 * jax.random.normal(next(ks), shape, f32)

    def small(shape, scale, offset=0.0):
        return offset + scale * jax.random.normal(next(ks), shape, f32)

    return {
        "x": jax.random.normal(next(ks), (BATCH, SEQ, D_MODEL), f32),
        "ffn1_norm": gain((L, D_MODEL)),
        "ffn1_w1": w((L, D_MODEL, D_FF), D_MODEL),
        "ffn1_w3": w((L, D_MODEL, D_FF), D_MODEL),
        "ffn1_w2": w((L, D_FF, D_MODEL), D_FF),
        "mix_norm": gain((L, D_MODEL)),
        "w_in": w((L, D_MODEL, D_IN), D_MODEL),
        "na_q_norm": gain((L, NA_HEAD_DIM)),
        "na_k_norm": gain((L, NA_HEAD_DIM)),
        "na_rpb": small((L, NA_HEADS, 2 * NA_WIN_R - 1, 2 * NA_WIN_C - 1), 0.1),
        "gla_gf_up": w((L, GLA_GATE_RANK, GLA_QK_W), GLA_GATE_RANK),
        "gla_gf_bias": small((L, GLA_QK_W), 0.1, 2.0),
        "gla_gb_up": w((L, GLA_GATE_RANK, GLA_QK_W), GLA_GATE_RANK),
        "gla_gb_bias": small((L, GLA_QK_W), 0.1, 2.0),
        "gla_out_norm": gain((L, GLA_DV)),
        "mla_cq_norm": gain((L, MLA_Q_RANK)),
        "mla_ckv_norm": gain((L, MLA_KV_RANK)),
        "mla_w_uq": w((L, MLA_Q_RANK, MLA_HEADS * MLA_QK_HEAD), MLA_Q_RANK),
        "mla_w_ukv": w((L, MLA_KV_RANK, MLA_HEADS * (MLA_NOPE + MLA_V)), MLA_KV_RANK),
        "mla_q_norm": gain((L, MLA_QK_HEAD)),
        "mla_k_norm": gain((L, MLA_QK_HEAD)),
        "w_br_na": w((L, NA_W, D_MODEL), NA_W),
        "w_br_gla": w((L, GLA_V_W, D_MODEL), GLA_V_W),
        "w_br_mla": w((L, MLA_V_W, D_MODEL), MLA_V_W),
        "w_out": w((L, D_MODEL, D_MODEL), D_MODEL),
        "ffn2_norm": gain((L, D_MODEL)),
        "ffn2_w1": w((L, D_MODEL, D_FF), D_MODEL),
        "ffn2_w3": w((L, D_MODEL, D_FF), D_MODEL),
        "ffn2_w2": w((L, D_FF, D_MODEL), D_FF),
    }


def reference(x, ffn1_norm, ffn1_w1, ffn1_w3, ffn1_w2, mix_norm, w_in,
              na_q_norm, na_k_norm, na_rpb,
              gla_gf_up, gla_gf_bias, gla_gb_up, gla_gb_bias, gla_out_norm,
              mla_cq_norm, mla_ckv_norm, mla_w_uq, mla_w_ukv, mla_q_norm, mla_k_norm,
              w_br_na, w_br_gla, w_br_mla, w_out,
              ffn2_norm, ffn2_w1, ffn2_w3, ffn2_w2):
    B, S, D = x.shape
    cos, sin = rope_tables(S)
    for l in range(DEPTH):
        x = x + 0.5 * swiglu(rms_norm(x, ffn1_norm[l]), ffn1_w1[l], ffn1_w3[l], ffn1_w2[l])

        h = rms_norm(x, mix_norm[l])
        z = h @ w_in[l]
        (na_q, na_k, na_v, gq, gk, gv, gr, gfl, gbl, c_q, c_kv, k_rope, gates) = split_cols(z, IN_SPLITS)

        qa = rms_norm(na_q.reshape(B, S, NA_HEADS, NA_HEAD_DIM), na_q_norm[l])
        ka = rms_norm(na_k.reshape(B, S, NA_HEADS, NA_HEAD_DIM), na_k_norm[l])
        va = na_v.reshape(B, S, NA_HEADS, NA_HEAD_DIM)
        y_na = neighborhood_attention(qa, ka, va, na_rpb[l])

        g_f = jax.nn.log_sigmoid((gfl @ gla_gf_up[l] + gla_gf_bias[l]).astype(jnp.float32)) / GLA_GATE_TAU
        g_b = jax.nn.log_sigmoid((gbl @ gla_gb_up[l] + gla_gb_bias[l]).astype(jnp.float32)) / GLA_GATE_TAU
        o_gla = gla_bidirectional(gq.reshape(B, S, GLA_HEADS, GLA_DK), gk.reshape(B, S, GLA_HEADS, GLA_DK),
                                  gv.reshape(B, S, GLA_HEADS, GLA_DV),
                                  g_f.reshape(B, S, GLA_HEADS, GLA_DK), g_b.reshape(B, S, GLA_HEADS, GLA_DK))
        o_gla = rms_norm(o_gla, gla_out_norm[l]).astype(x.dtype).reshape(B, S, GLA_V_W)
        y_gla = o_gla * jax.nn.silu(gr)

        y_mla = mla_attention(c_q, c_kv, k_rope, mla_cq_norm[l], mla_ckv_norm[l], mla_w_uq[l], mla_w_ukv[l],
                              mla_q_norm[l], mla_k_norm[l], cos, sin)

        gt = jax.nn.sigmoid(gates.reshape(B, S, N_BRANCH, D))
        mixed = (gt[:, :, 0] * (y_na @ w_br_na[l])
                 + gt[:, :, 1] * (y_gla @ w_br_gla[l])
                 + gt[:, :, 2] * (y_mla @ w_br_mla[l]))
        x = x + mixed @ w_out[l]

        x = x + 0.5 * swiglu(rms_norm(x, ffn2_norm[l]), ffn2_w1[l], ffn2_w3[l], ffn2_w2[l])
    return x
```

```python
import contextlib
import numpy as np
import concourse.bass as bass
import concourse.mybir as mybir
from concourse.bass_utils import run_bass_kernel_spmd

F32 = mybir.dt.float32
BF16 = mybir.dt.bfloat16
AF = mybir.ActivationFunctionType
ALU = mybir.AluOpType
EPS = 1e-6
SAME_ENGINE_RAW = True


class Res:
    __slots__ = ("name", "writer", "readers")

    def __init__(self, name=""):
        self.name = name
        self.writer = None
        self.readers = []


class Op:
    __slots__ = ("eng", "fn", "deps", "needed", "is_dma", "sem", "val", "dsem", "rawdeps")

    def __init__(self, eng, fn, is_dma, dsem):
        self.eng = eng
        self.fn = fn
        self.deps = []
        self.needed = False
        self.is_dma = is_dma
        self.dsem = dsem
        self.sem = None
        self.val = None


class Sched:
    ENG = ("pe", "act", "dve", "pool", "sp")

    def __init__(self, nc):
        self.nc = nc
        self.q = {e: [] for e in self.ENG}
        self.all = []
        self.key_eng = {}

    def op(self, eng, fn, reads=(), writes=(), dma=None):
        o = Op(eng, fn, dma is not None, dma)
        if dma is not None:
            assert self.key_eng.setdefault(dma, eng) == eng, dma
        deps = []
        raw = set()
        for r in reads:
            w = r.writer
            if w is not None:
                deps.append(w)
                raw.add(id(w))
        for r in writes:
            w = r.writer
            if w is not None:
                deps.append(w)
            deps.extend(r.readers)
        seen = set()
        for d in deps:
            if id(d) in seen or d is o:
                continue
            seen.add(id(d))
            if (not d.is_dma) and d.eng == eng:
                if eng == "pe" or not SAME_ENGINE_RAW or id(d) not in raw:
                    continue
            o.deps.append(d)
            d.needed = True
        for r in reads:
            r.readers.append(o)
        for r in writes:
            r.writer = o
            r.readers = []
        self.q[eng].append(o)
        self.all.append(o)
        return o

    def wait_ops(self, eng, ops):
        o = Op(eng, None, False, None)
        for d in ops:
            if d is None:
                continue
            o.deps.append(d)
            d.needed = True
        self.q[eng].append(o)
        self.all.append(o)
        return o

    def barrier(self):
        lasts = []
        for e in self.ENG:
            for o in reversed(self.q[e]):
                if o.fn is not None and not o.is_dma:
                    lasts.append(o)
                    break
        lastdma = {}
        for o in self.all:
            if o.is_dma:
                lastdma[o.dsem] = o
        deps = lasts + list(lastdma.values())
        for e in self.ENG:
            if self.q[e]:
                self.wait_ops(e, [d for d in deps if d.is_dma or d.eng != e])

    def finalize(self):
        nc = self.nc
        with contextlib.ExitStack() as st:
            esem = {e: st.enter_context(nc.semaphore("s_" + e)) for e in self.ENG}
            dkeys = []
            for o in self.all:
                if o.is_dma and o.dsem not in dkeys:
                    dkeys.append(o.dsem)
            dsem = {k: st.enter_context(nc.semaphore("d_%s" % (k,))) for k in dkeys}
            cnt = {e: 0 for e in self.ENG}
            dcnt = {k: 0 for k in dkeys}
            for e in self.ENG:
                for o in self.q[e]:
                    if o.is_dma:
                        dcnt[o.dsem] += 16
                        o.sem = dsem[o.dsem]
                        o.val = dcnt[o.dsem]
                    elif o.needed:
                        cnt[e] += 1
                        o.sem = esem[e]
                        o.val = cnt[e]
            block = st.enter_context(nc.Block())
            engobj = {"pe": "tensor", "act": "scalar", "dve": "vector", "pool": "gpsimd", "sp": "sync"}

            def make(e):
                ops = self.q[e]

                def body(eng):
                    waited = {}
                    for o in ops:
                        for d in o.deps:
                            key = id(d.sem)
                            if waited.get(key, 0) >= d.val:
                                continue
                            waited[key] = d.val
                            eng.wait_ge(d.sem, d.val)
                        if o.fn is None:
                            continue
                        ins = o.fn(eng)
                        if o.is_dma:
                            ins.then_inc(o.sem, 16)
                        elif o.needed:
                            ins.then_inc(o.sem, 1)
                return body

            for e in self.ENG:
                if self.q[e]:
                    getattr(block, engobj[e])(make(e))


class Buf:
    def __init__(self, t, name):
        self.t = t
        self.r = Res(name)
        self.name = name


class Rot:
    def __init__(self, bufs):
        self.b = bufs
        self.i = 0

    def next(self):
        b = self.b[self.i % len(self.b)]
        self.i += 1
        return b


class K:
    def __init__(self, nc, st):
        self.nc = nc
        self.st = st
        self.S = Sched(nc)
        self.ci = 0
        self.outs = []

    def sb(self, name, shape, dt):
        return Buf(self.st.enter_context(self.nc.sbuf_tensor(name, shape, dt)), name)

    def rot(self, name, n, shape, dt):
        return Rot([self.sb("%s%d" % (name, i), shape, dt) for i in range(n)])

    def psum(self, n=8):
        self.dbl = []
        bufs = []
        for i in range(n // 2):
            t = self.st.enter_context(self.nc.psum_tensor("psd%d" % i, [128, 1024], F32))
            b0 = Buf(t[:, 0:512], "ps%d" % (2 * i))
            b1 = Buf(t[:, 512:1024], "ps%d" % (2 * i + 1))
            bufs += [b0, b1]
            self.dbl.append((t, b0.r, b1.r))
        return Rot(bufs)

    def dma(self, out, in_, reads, writes, key, eng="sp", is_out=False):
        o = self.S.op(eng, lambda e: e.dma_start(out=out, in_=in_), reads=reads, writes=writes, dma=key)
        if is_out:
            self.outs.append(o)
        return o

    def mm(self, out, lhsT, rhs, start, stop, reads, writes):
        return self.S.op("pe", lambda e: e.matmul(out, lhsT=lhsT, rhs=rhs, start=start, stop=stop),
                         reads=reads, writes=writes)

    def act(self, out, in_, func, reads, writes, scale=1.0, bias=0.0):
        return self.S.op("act", lambda e: e.activation(out=out, in_=in_, func=func, bias=bias, scale=scale),
                         reads=reads, writes=writes)

    def tt(self, eng, out, in0, in1, op, reads, writes):
        return self.S.op(eng, lambda e: e.tensor_tensor(out=out, in0=in0, in1=in1, op=op), reads=reads, writes=writes)

    def ts(self, eng, out, in0, s1, s2, op0, op1, reads, writes):
        if s2 is None:
            return self.S.op(eng, lambda e: e.tensor_single_scalar(out=out, in_=in0, scalar=s1, op=op0),
                             reads=reads, writes=writes)
        return self.S.op(eng, lambda e: e.tensor_scalar(out=out, in0=in0, scalar1=s1, scalar2=s2, op0=op0, op1=op1),
                         reads=reads, writes=writes)

    def stt(self, eng, out, in0, scalar, in1, op0, op1, reads, writes):
        return self.S.op(eng, lambda e: e.scalar_tensor_tensor(out=out, in0=in0, scalar=scalar, in1=in1, op0=op0, op1=op1),
                         reads=reads, writes=writes)

    def copy(self, eng, out, in_, reads, writes):
        if eng == "act":
            return self.S.op("act", lambda e: e.copy(out=out, in_=in_), reads=reads, writes=writes)
        return self.S.op(eng, lambda e: e.tensor_copy(out=out, in_=in_), reads=reads, writes=writes)

    def memset(self, eng, ap, val, writes):
        return self.S.op(eng, lambda e: e.memset(ap, val), writes=writes)

    def cast_eng(self, engs=("pool", "dve", "pool", "act")):
        e = engs[self.ci % len(engs)]
        self.ci += 1
        return e

    def finish(self):
        self.S.wait_ops("sp", self.outs)
        self.S.finalize()


TOK = 2048
NTB = 4
DFF = 2816
DIN = 6752
ZT_COLS = [(1024, 512), (1792, 256), (2048, 512)]
NZT = 1280
WSLOT = 2816


class AProg:
    def __init__(self, do_k3, do_k1):
        self.do_k3, self.do_k1 = do_k3, do_k1
        nc = bass.Bass("TRN2", target_bir_lowering=False)
        self.nc = nc
        din = lambda n, s: nc.dram_tensor(n, s, F32, kind="ExternalInput").ap()
        dout = lambda n, s: nc.dram_tensor(n, s, F32, kind="ExternalOutput").ap()
        self.xin = din("xin", [1024, TOK])
        self.pvec = din("pvec", [128, 24])
        if do_k3:
            self.gates = din("gates", [3072, TOK])
            self.yin = nc.dram_tensor("yin", [1536, TOK], BF16, kind="ExternalInput").ap()
            self.wbr = din("wbr", [1536, 1024])
            self.wout = din("wout", [1024, 1024])
            self.f2 = [din("f2w1", [1024, DFF]), din("f2w3", [1024, DFF]), din("f2w2", [DFF, 1024])]
        if do_k1:
            self.f1 = [din("f1w1", [1024, DFF]), din("f1w3", [1024, DFF]), din("f1w2", [DFF, 1024])]
            self.win = din("win", [1024, DIN])
            self.zT = dout("zT", [DIN, TOK])
            self.ztok = dout("ztok", [TOK, NZT])
        self.xout = dout("xout", [1024, TOK])
        with contextlib.ExitStack() as st:
            self.k = K(nc, st)
            self.build()
            self.k.finish()

    def load_w(self, view, kc, ncols):
        k = self.k
        n = kc * ncols
        assert n <= WSLOT
        stg = self.stage.next()
        wb = self.wb.next()
        sv = stg.t[:, 0:n].rearrange("p (a b) -> p a b", a=kc)
        k.dma(sv, view, reads=[], writes=[stg.r], key=stg.name)
        k.copy(k.cast_eng(), wb.t[:, 0:n], stg.t[:, 0:n], reads=[stg.r], writes=[wb.r])
        return wb.t[:, 0:n].rearrange("p (a b) -> p a b", a=kc), wb.r

    def rmsnorm(self, gcol, hview, hres, tbs):
        k = self.k
        for j, tb in enumerate(tbs):
            pb = self.ps.next()
            for kc in range(8):
                sq = self.sq.next()
                k.act(sq.t[:], self.X.t[:, kc, tb * 512:(tb + 1) * 512], AF.Square, reads=[self.Xr[kc][tb]], writes=[sq.r])
                k.mm(pb.t[:], self.ones.t[:], sq.t[:], kc == 0, kc == 7, reads=[sq.r, self.ones.r], writes=[pb.r])
            rs = self.tmpf.next()
            k.act(rs.t[:], pb.t[:], AF.Ln, reads=[pb.r], writes=[rs.r], scale=1.0 / 1024.0, bias=self.epsb.t[:, 0:1])
            k.act(rs.t[:], rs.t[:], AF.Exp, reads=[rs.r], writes=[rs.r], scale=-0.5)
            for kc in range(8):
                k.stt("dve", hview(kc, j), self.X.t[:, kc, tb * 512:(tb + 1) * 512], self.pv.t[:, gcol + kc:gcol + kc + 1],
                      rs.t[:], ALU.mult, ALU.mult, reads=[self.Xr[kc][tb], rs.r, self.pv.r], writes=[hres[kc][j]])

    def ffn(self, gcol, w1, w3, w2):
        k = self.k
        A = self.arena.t
        hres = [[Res("h") for _ in range(2)] for _ in range(8)]
        gres = [[Res("g") for _ in range(2)] for _ in range(22)]
        hv = lambda kc, t: A[:, kc * 1024 + t * 512: kc * 1024 + (t + 1) * 512]
        gv = lambda fc, t: A[:, 8192 + fc * 1024 + t * 512: 8192 + fc * 1024 + (t + 1) * 512]
        for half in range(2):
            self.rmsnorm(gcol, hv, hres, [half * 2, half * 2 + 1])
            for fp in range(11):
                w1b, r1 = self.load_w(w1[:, fp * 256:(fp + 1) * 256].rearrange("(a p) n -> p a n", p=128), 8, 256)
                w3b, r3 = self.load_w(w3[:, fp * 256:(fp + 1) * 256].rearrange("(a p) n -> p a n", p=128), 8, 256)
                for sub in range(2):
                    fc = fp * 2 + sub
                    for t in range(2):
                        pu = self.ps.next()
                        for kc in range(8):
                            k.mm(pu.t[:], w1b[:, kc, sub * 128:(sub + 1) * 128], hv(kc, t), kc == 0, kc == 7,
                                 reads=[r1, hres[kc][t]], writes=[pu.r])
                        pv = self.ps.next()
                        for kc in range(8):
                            k.mm(pv.t[:], w3b[:, kc, sub * 128:(sub + 1) * 128], hv(kc, t), kc == 0, kc == 7,
                                 reads=[r3, hres[kc][t]], writes=[pv.r])
                        s = self.tmpf.next()
                        k.act(s.t[:], pu.t[:], AF.Silu, reads=[pu.r], writes=[s.r])
                        k.tt("dve", gv(fc, t), s.t[:], pv.t[:], ALU.mult, reads=[s.r, pv.r], writes=[gres[fc][t]])
            for dc in range(8):
                w2b, r2 = self.load_w(w2[:, dc * 128:(dc + 1) * 128].rearrange("(a p) n -> p a n", p=128), 22, 128)
                for t in range(2):
                    tb = half * 2 + t
                    py = self.ps.next()
                    for fc in range(22):
                        k.mm(py.t[:], w2b[:, fc, :], gv(fc, t), fc == 0, fc == 21, reads=[r2, gres[fc][t]], writes=[py.r])
                    xs = self.X.t[:, dc, tb * 512:(tb + 1) * 512]
                    k.stt("dve", xs, py.t[:], 0.5, xs, ALU.mult, ALU.add, reads=[py.r, self.Xr[dc][tb]], writes=[self.Xr[dc][tb]])

    def k3(self):
        k = self.k
        A = self.arena.t
        ybf = lambda c, tb: A[:, c * 2048 + tb * 512: c * 2048 + (tb + 1) * 512]
        mbf = lambda c: A[:, 24576 + c * 512: 24576 + (c + 1) * 512]
        yres = [Res("y") for _ in range(12)]
        mres = [Res("m") for _ in range(8)]
        for c in range(12):
            k.dma(A[:, c * 2048:(c + 1) * 2048], self.yin[c * 128:(c + 1) * 128, :], reads=[], writes=[yres[c]], key="yin%d" % c)
        for tb in range(NTB):
            ts_ = slice(tb * 512, (tb + 1) * 512)
            for dc in range(8):
                W, rw = self.load_w(self.wbr[:, dc * 128:(dc + 1) * 128].rearrange("(a p) n -> p a n", p=128), 12, 128)
                macc = self.macc.next()
                for i in range(3):
                    gs = self.gpool.next()
                    k.dma(gs.t[:], self.gates[i * 1024 + dc * 128: i * 1024 + (dc + 1) * 128, ts_], reads=[], writes=[gs.r], key=gs.name)
                    ps = self.ps.next()
                    for c in range(4):
                        k.mm(ps.t[:], W[:, i * 4 + c, :], ybf(i * 4 + c, tb), c == 0, c == 3, reads=[rw, yres[i * 4 + c]], writes=[ps.r])
                    sg = self.tmpf.next()
                    k.act(sg.t[:], gs.t[:], AF.Sigmoid, reads=[gs.r], writes=[sg.r])
                    if i == 0:
                        k.tt("dve", macc.t[:], sg.t[:], ps.t[:], ALU.mult, reads=[sg.r, ps.r], writes=[macc.r])
                    else:
                        k.tt("dve", sg.t[:], sg.t[:], ps.t[:], ALU.mult, reads=[sg.r, ps.r], writes=[sg.r])
                        if i == 1:
                            k.tt("dve", macc.t[:], macc.t[:], sg.t[:], ALU.add, reads=[macc.r, sg.r], writes=[macc.r])
                        else:
                            k.tt("dve", mbf(dc), macc.t[:], sg.t[:], ALU.add, reads=[macc.r, sg.r], writes=[mres[dc]])
            for dc in range(8):
                W, rw = self.load_w(self.wout[:, dc * 128:(dc + 1) * 128].rearrange("(a p) n -> p a n", p=128), 8, 128)
                ps = self.ps.next()
                for c in range(8):
                    k.mm(ps.t[:], W[:, c, :], mbf(c), c == 0, c == 7, reads=[rw, mres[c]], writes=[ps.r])
                xs = self.X.t[:, dc, ts_]
                k.tt("dve", xs, ps.t[:], xs, ALU.add, reads=[ps.r, self.Xr[dc][tb]], writes=[self.Xr[dc][tb]])

    def zphase(self):
        k = self.k
        A = self.arena.t
        hres = [[Res("h2") for _ in range(4)] for _ in range(8)]
        hv = lambda kc, tb: A[:, kc * 2048 + tb * 512: kc * 2048 + (tb + 1) * 512]
        self.rmsnorm(16, hv, hres, [0, 1, 2, 3])
        ev = 0
        for fp in range(27):
            if fp in (4, 5, 8, 9):
                continue
            c0 = fp * 256
            ncols = min(256, DIN - c0)
            W, rw = self.load_w(self.win[:, c0:c0 + ncols].rearrange("(a p) n -> p a n", p=128), 8, ncols)
            for sub in range((ncols + 127) // 128):
                m = min(128, ncols - sub * 128)
                for tb in range(NTB):
                    ps = self.ps.next()
                    for kc in range(8):
                        k.mm(ps.t[0:m, :], W[:, kc, sub * 128: sub * 128 + m], hv(kc, tb), kc == 0, kc == 7,
                             reads=[rw, hres[kc][tb]], writes=[ps.r])
                    zo = self.tmpf.next()
                    k.copy(("act", "dve")[ev % 2], zo.t[0:m, :], ps.t[0:m, :], reads=[ps.r], writes=[zo.r])
                    ev += 1
                    f0 = c0 + sub * 128
                    k.dma(self.zT[f0:f0 + m, tb * 512:(tb + 1) * 512], zo.t[0:m, :], reads=[zo.r], writes=[], key=zo.name + "o", eng="act", is_out=True)
        off = 0
        for (c0, n) in ZT_COLS:
            for b in range(n // 256):
                W, rw = self.load_w(self.win[:, c0 + b * 256: c0 + (b + 1) * 256].rearrange("(a p) n -> p a n", p=128), 8, 256)
                for tc in range(16):
                    ps = self.ps.next()
                    for kc in range(8):
                        k.mm(ps.t[:, 0:256], A[:, kc * 2048 + tc * 128: kc * 2048 + (tc + 1) * 128], W[:, kc, :], kc == 0, kc == 7,
                             reads=[rw, hres[kc][tc // 4]], writes=[ps.r])
                    zo = self.tmpf.next()
                    k.copy(("act", "dve")[ev % 2], zo.t[:, 0:256], ps.t[:, 0:256], reads=[ps.r], writes=[zo.r])
                    ev += 1
                    k.dma(self.ztok[tc * 128:(tc + 1) * 128, off:off + 256], zo.t[:, 0:256], reads=[zo.r], writes=[], key=zo.name + "o", eng="act", is_out=True)
                off += 256

    def build(self):
        k = self.k
        self.X = k.sb("X", [128, 8, TOK], F32)
        self.Xr = [[Res("X") for _ in range(NTB)] for _ in range(8)]
        self.arena = k.sb("arena", [128, 30720], BF16)
        self.stage = k.rot("stg", 3, [128, WSLOT], F32)
        self.wb = k.rot("wb", 4, [128, WSLOT], BF16)
        self.tmpf = k.rot("tmpf", 6, [128, 512], F32)
        self.macc = k.rot("macc", 2, [128, 512], F32)
        self.gpool = k.rot("gp", 4, [128, 512], F32)
        self.sq = k.rot("sq", 2, [128, 512], BF16)
        self.ones = k.sb("ones", [128, 128], BF16)
        self.pv = k.sb("pv", [128, 24], F32)
        self.epsb = k.sb("epsb", [128, 1], F32)
        k.memset("pool", self.epsb.t[:], EPS, writes=[self.epsb.r])
        self.ps = k.psum(8)
        k.memset("pool", self.ones.t[:], 1.0, writes=[self.ones.r])
        k.dma(self.pv.t[:], self.pvec[:, :], reads=[], writes=[self.pv.r], key="pv")
        for dc in range(8):
            k.dma(self.X.t[:, dc, :], self.xin[dc * 128:(dc + 1) * 128, :], reads=[], writes=self.Xr[dc], key="xin%d" % dc)
        if self.do_k3:
            self.k3()
            k.S.barrier()
            self.ffn(0, *self.f2)
            k.S.barrier()
        if self.do_k1:
            self.ffn(8, *self.f1)
            k.S.barrier()
            self.zphase()
        for dc in range(8):
            k.dma(self.xout[dc * 128:(dc + 1) * 128, :], self.X.t[:, dc, :], reads=self.Xr[dc], writes=[], key="xo%d" % dc, is_out=True)


SEQ = 4096
NQB = 8
NA_CLS_W = 320
NA_TAB = 5 * 640


def na_pair_info(t):
    if t < 2:
        return 0, 4, 1 + t
    if t >= 30:
        return 28, 4, 3 + (t - 30)
    return t - 2, 5, 0


class BProg:
    def __init__(self, parts=("na", "gla", "mla")):
        nc = bass.Bass("TRN2", target_bir_lowering=False)
        self.nc = nc
        self.parts = parts
        din = lambda n, s: nc.dram_tensor(n, s, F32, kind="ExternalInput").ap()
        self.naq = din("naq", [256, SEQ]); self.nak = din("nak", [256, SEQ]); self.nav = din("nav", [SEQ, 256])
        self.bcat = din("bcat", [4, 128, NA_TAB])
        self.gq = din("gq", [128, SEQ]); self.gk = din("gk", [128, SEQ]); self.gkt = din("gkt", [SEQ, 128])
        self.gvt = din("gvt", [SEQ, 256]); self.gr = din("gr", [256, SEQ])
        self.gff = din("gff", [32, SEQ]); self.gfb = din("gfb", [32, SEQ])
        self.upf = din("upf", [32, 128]); self.upb = din("upb", [32, 128])
        self.tri = din("tri", [128, 6, 128])
        self.cq = din("cq", [256, SEQ]); self.ckv = din("ckv", [256, SEQ]); self.kro = din("kro", [128, SEQ])
        self.cs = din("cs", [128, SEQ])
        self.wuq = din("wuq", [2, 256, 256]); self.wukv = din("wukv", [2, 256, 256])
        self.pvec = din("pvec", [128, 16])
        self.yT = nc.dram_tensor("yT", [768, SEQ], BF16, kind="ExternalOutput").ap()
        with contextlib.ExitStack() as st:
            self.k = K(nc, st)
            self.build()
            self.k.finish()

    def build(self):
        k = self.k
        self.AF = k.sb("AF", [128, 13952], F32)
        self.AB = k.sb("AB", [128, 36864], BF16)
        self.tmpf = k.rot("tmpf", 8, [128, 512], F32)
        self.tmpb = k.rot("tmpb", 4, [128, 512], BF16)
        self.ones = k.sb("ones", [128, 128], BF16)
        self.bd = k.sb("bd", [128, 128], BF16)
        self.pv = k.sb("pv", [128, 16], F32)
        self.cst = k.sb("cst", [128, 8], F32)
        self.trit = k.sb("trit", [128, 6, 128], F32)
        banks = k.psum(8)
        self.banks = banks.b
        self.ps = Rot(banks.b[0:5])
        self.acc = banks.b[5:8]
        k.memset("pool", self.ones.t[:], 1.0, writes=[self.ones.r])
        k.memset("pool", self.bd.t[:], 0.0, writes=[self.bd.r])
        k.memset("pool", self.bd.t[0:64, 0:64], 1.0, writes=[self.bd.r])
        k.memset("pool", self.bd.t[64:128, 64:128], 1.0, writes=[self.bd.r])
        for j, v in enumerate((EPS, 64.0 * EPS, 192.0 * EPS, 1.0)):
            k.memset("pool", self.cst.t[:, j:j + 1], v, writes=[self.cst.r])
        k.dma(self.pv.t[:], self.pvec[:, :], [], [self.pv.r], "pv")
        k.dma(self.trit.t[:], self.tri[:, :, :], [], [self.trit.r], "tri")
        if "na" in self.parts:
            self.na()
            k.S.barrier()
        if "gla" in self.parts:
            self.gla()
            k.S.barrier()
        if "mla" in self.parts:
            self.mla()

    ybase = (0, 256, 512)

    def load_gfl(self, g, d, tb):
        src = (self.gff, self.gfb)[d]
        self.k.dma(g.t[0:32, :], src[:, tb * 512:(tb + 1) * 512], [], [g.r], g.name)

    def load_kro(self, buf, tb, swapped):
        r0 = 64 if swapped else 0
        self.k.dma(buf.t[0:64, :], self.kro[r0:r0 + 64, tb * 512:(tb + 1) * 512], [], [buf.r], buf.name)

    def na_pt(self):
        if not hasattr(self, "_npt"):
            self._npt = self.k.rot("napt", 3, [128, 640], BF16)
        return self._npt

    def gla_small(self):
        if not hasattr(self, "_gs"):
            k = self.k
            self._gs = (k.sb("dec", [128, 2, 32], F32), k.sb("gup", [32, 2, 128], F32), k.sb("gupb", [32, 2, 128], BF16),
                        k.sb("Sf", [128, 128], F32), k.sb("Sbk", [128, 128], F32), k.rot("Sfb", 2, [128, 128], BF16))
        return self._gs

    def mla_small(self):
        if not hasattr(self, "_ms"):
            self._ms = (self.k.sb("wq", [128, 2, 2, 256], BF16), self.k.sb("wkv", [128, 2, 2, 256], BF16),
                        self.k.rot("sacc", 2, [128, 512], F32), self.k.sb("onesf", [128, 128], F32))
        return self._ms

    def rstd(self, out, ps_ap, scale, cst_col, reads, wres, np_=128):
        k = self.k
        k.act(out, ps_ap, AF.Ln, reads=reads + [self.cst.r], writes=[wres], scale=scale, bias=self.cst.t[0:np_, cst_col:cst_col + 1])
        k.act(out, out, AF.Exp, reads=[wres], writes=[wres], scale=-0.5)

    def na(self):
        k = self.k
        AB, AFt = self.AB.t, self.AF.t
        qn = lambda sl, c0, n: AB[sl, c0:c0 + n]
        kn = lambda sl, c0, n: AB[sl, 4096 + c0: 4096 + c0 + n]
        V = AB[:, 8192:16384].rearrange("p (c f) -> p c f", f=256)
        rV = Res("naV")
        for c4 in range(8):
            for hh in range(2):
                st_ = self.tmpf.next()
                sv = st_.t[:, :].rearrange("p (c f) -> p c f", f=128)
                k.dma(sv, self.nav[c4 * 512:(c4 + 1) * 512, hh * 128:(hh + 1) * 128].rearrange("(c p) f -> p c f", p=128), [], [st_.r], st_.name)
                k.copy(k.cast_eng(("pool", "dve")), V[:, c4 * 4:(c4 + 1) * 4, hh * 128:(hh + 1) * 128], sv, [st_.r], [rV])
        for hp in range(2):
            rq = [Res("naq") for _ in range(NQB)]
            rk = [Res("nak") for _ in range(NQB)]
            for (src, dst, res, gcol, scale, ccol) in ((self.naq, qn, rq, 0, 1.0, 1), (self.nak, kn, rk, 1, 1.0 / 64.0, 0)):
                for tb in range(NQB):
                    x = self.tmpf.next()
                    k.dma(x.t[:], src[hp * 128:(hp + 1) * 128, tb * 512:(tb + 1) * 512], [], [x.r], x.name)
                    sq = self.tmpb.next()
                    k.act(sq.t[:], x.t[:], AF.Square, [x.r], [sq.r])
                    pb = self.ps.next()
                    k.mm(pb.t[:], self.bd.t[:], sq.t[:], True, True, [sq.r, self.bd.r], [pb.r])
                    rs = self.tmpf.next()
                    self.rstd(rs.t[:], pb.t[:], scale, ccol, [pb.r], rs.r)
                    k.stt("dve", dst(slice(0, 128), tb * 512, 512), x.t[:], self.pv.t[:, gcol:gcol + 1], rs.t[:], ALU.mult, ALU.mult,
                          [x.r, rs.r, self.pv.r], [res[tb]])
            if not hasattr(self, "bcres"):
                self.bcres = [Res("bc0"), Res("bc1")]
                self.yres = [Res("Y0"), Res("Y1")]
            dq = Rot([0, 1])
            hd = []
            for hl in range(2):
                h = hp * 2 + hl
                bct = AFt[:, hl * NA_TAB:(hl + 1) * NA_TAB]
                k.dma(bct, self.bcat[h, :, :], [], [self.bcres[hl]], "bcat%d" % hl)
                Y = AFt[0:64, 2 * NA_TAB + hl * 2048: 2 * NA_TAB + (hl + 1) * 2048].bitcast(BF16)
                hd.append((h, slice(hl * 64, (hl + 1) * 64), bct, self.bcres[hl], Y, self.yres[hl], [self.banks[4 + 2 * hl], self.banks[5 + 2 * hl]]))
            units = [(t, hl) for t in range(32) for hl in range(2)]
            npt = self.na_pt()

            def qk(u):
                t, hl = units[u]
                h, hs, bct, bcr, Y, yr, obs = hd[hl]
                a0, ns, cls = na_pair_info(t)
                d = k.dbl[dq.next()]
                for s_ in range(ns):
                    k.mm(d[0][:, s_ * 128:(s_ + 1) * 128], kn(hs, (a0 + s_) * 128, 128), qn(hs, t * 128, 128), True, True,
                         [rk[(a0 + s_) // 4], rq[t // 4]], [d[1], d[2]])
                return d

            DEPTH = 1
            pend = [qk(u) for u in range(DEPTH)]
            for u, (t, hl) in enumerate(units):
                h, hs, bct, bcr, Y, yr, obs = hd[hl]
                t2, j = divmod(t, 2)
                ob = obs[t2 % 2]
                a0, ns, cls = na_pair_info(t)
                n = ns * 128
                d = pend.pop(0)
                if u + DEPTH < len(units):
                    pend.append(qk(u + DEPTH))
                k.tt("dve", d[0][:, 0:n], d[0][:, 0:n], bct[:, cls * 640: cls * 640 + n], ALU.add, [d[1], d[2], bcr], [d[1], d[2]])
                pT = npt.next()
                k.act(pT.t[:, 0:n], d[0][:, 0:n], AF.Exp, [d[1], d[2]], [pT.r])
                for s_ in range(ns):
                    k.mm(ob.t[0:64, j * 256: j * 256 + 128], V[:, a0 + s_, h * 64:(h + 1) * 64], pT.t[:, s_ * 128:(s_ + 1) * 128],
                         s_ == 0, s_ == ns - 1, [rV, pT.r], [ob.r])
                for s_ in range(ns):
                    k.mm(ob.t[0:64, j * 256 + 128: j * 256 + 256], self.ones.t[:, 0:64], pT.t[:, s_ * 128:(s_ + 1) * 128],
                         s_ == 0, s_ == ns - 1, [self.ones.r, pT.r], [ob.r])
                if j == 1:
                    ov = ob.t[0:64, :].rearrange("p (j t q) -> p j t q", j=2, t=2)
                    rc = self.tmpf.next()
                    rcv = rc.t[0:64, 0:256].rearrange("p (j q) -> p j q", j=2)
                    k.S.op("dve", lambda e, o=rcv, i=ov[:, :, 1, :]: e.reciprocal(out=o, in_=i), reads=[ob.r], writes=[rc.r])
                    k.tt("dve", Y[:, t2 * 256:(t2 + 1) * 256].rearrange("p (j q) -> p j q", j=2), ov[:, :, 0, :], rcv, ALU.mult,
                         [ob.r, rc.r], [yr])
            for hl in range(2):
                h, hs, bct, bcr, Y, yr, obs = hd[hl]
                k.dma(self.yT[self.ybase[0] + h * 64: self.ybase[0] + (h + 1) * 64, :], Y, [yr], [], "yna%d" % hl, is_out=True)

    def gla(self):
        k = self.k
        AB, AFt = self.AB.t, self.AF.t
        TR = self.trit.t
        sp = [AFt[:, d * 4096:(d + 1) * 4096].rearrange("p (c f) -> p c f", f=128) for d in range(2)]
        rsp = [[Res("sp") for _ in range(8)] for _ in range(2)]
        qe = [AB[:, d * 4096:(d + 1) * 4096] for d in range(2)]
        ke = [AB[:, 8192 + d * 4096: 8192 + (d + 1) * 4096] for d in range(2)]
        kend = [AB[:, 16384 + d * 4096: 16384 + (d + 1) * 4096].rearrange("p (c f) -> p c f", f=128) for d in range(2)]
        V = AB[:, 24576:32768].rearrange("p (c f) -> p c f", f=256)
        Sb = AB[:, 32768:36864].rearrange("p (c f) -> p c f", f=128)
        rqe = [[Res("qe") for _ in range(8)] for _ in range(2)]
        rke = [[Res("ke") for _ in range(8)] for _ in range(2)]
        rkend = [[Res("kend") for _ in range(8)] for _ in range(2)]
        rV = [Res("gV") for _ in range(8)]
        rSb = [Res("Sb") for _ in range(32)]
        dec, up, upb_, Sf, Sbk, Sfb = self.gla_small()
        rdec = [[Res("dec") for _ in range(8)] for _ in range(2)]
        k.dma(up.t[:, 0, :], self.upf[:, :], [], [up.r], "gup")
        k.dma(up.t[:, 1, :], self.upb[:, :], [], [up.r], "gup")
        k.copy("dve", upb_.t[:], up.t[:], [up.r], [upb_.r])
        for c4 in range(8):
            for hh in range(2):
                st_ = self.tmpf.next()
                sv = st_.t[:, :].rearrange("p (c f) -> p c f", f=128)
                k.dma(sv, self.gvt[c4 * 512:(c4 + 1) * 512, hh * 128:(hh + 1) * 128].rearrange("(c p) f -> p c f", p=128), [], [st_.r], st_.name)
                k.copy(k.cast_eng(("pool", "dve")), V[:, c4 * 4:(c4 + 1) * 4, hh * 128:(hh + 1) * 128], sv, [st_.r], [rV[c4]])
        for d in range(2):
            for tb in range(8):
                g = self.tmpf.next()
                self.load_gfl(g, d, tb)
                gb = self.tmpb.next()
                k.copy("dve", gb.t[0:32, :], g.t[0:32, :], [g.r], [gb.r])
                ps = self.ps.next()
                for c in range(4):
                    k.mm(ps.t[:, c * 128:(c + 1) * 128], gb.t[0:32, c * 128:(c + 1) * 128], upb_.t[:, d, :], True, True, [gb.r, upb_.r], [ps.r])
                e1 = self.tmpf.next()
                k.act(e1.t[:], ps.t[:], AF.Exp, [ps.r], [e1.r], scale=-1.0)
                k.act(sp[d][:, tb * 4:(tb + 1) * 4, :], e1.t[:].rearrange("p (c f) -> p c f", f=128), AF.Ln, [e1.r, self.cst.r], [rsp[d][tb]],
                      bias=self.cst.t[:, 3:4])
        for tb in range(8):
            xq = self.tmpf.next()
            k.dma(xq.t[:], self.gq[:, tb * 512:(tb + 1) * 512], [], [xq.r], xq.name)
            xk = self.tmpf.next()
            k.dma(xk.t[:], self.gk[:, tb * 512:(tb + 1) * 512], [], [xk.r], xk.name)
            xkt = self.tmpf.next()
            xktv = xkt.t[:, :].rearrange("p (c f) -> p c f", f=128)
            k.dma(xktv, self.gkt[tb * 512:(tb + 1) * 512, :].rearrange("(c p) f -> p c f", p=128), [], [xkt.r], xkt.name)
            for d in range(2):
                pb = self.ps.next()
                pe_ = self.ps.next()
                for c in range(4):
                    k.mm(pb.t[:, c * 128:(c + 1) * 128], sp[d][:, tb * 4 + c, :], TR[:, 2 * d, :], True, True, [rsp[d][tb], self.trit.r], [pb.r])
                    k.mm(pe_.t[:, c * 128:(c + 1) * 128], TR[:, 2 * d + 1, :], sp[d][:, tb * 4 + c, :], True, True, [rsp[d][tb], self.trit.r], [pe_.r])
                eb = self.tmpf.next()
                k.act(eb.t[:], pb.t[:], AF.Exp, [pb.r], [eb.r], scale=-1.0 / 16.0)
                col = 127 if d == 0 else 0
                k.copy("pool", dec.t[:, d, tb * 4:(tb + 1) * 4], eb.t[:, :].rearrange("p (c t) -> p c t", t=128)[:, :, col], [eb.r], [rdec[d][tb]])
                k.stt("dve", qe[d][:, tb * 512:(tb + 1) * 512], xq.t[:], 0.125, eb.t[:], ALU.mult, ALU.mult, [xq.r, eb.r], [rqe[d][tb]])
                ei = self.tmpf.next()
                k.act(ei.t[:], pb.t[:], AF.Exp, [pb.r], [ei.r], scale=1.0 / 16.0)
                k.tt("dve", ke[d][:, tb * 512:(tb + 1) * 512], xk.t[:], ei.t[:], ALU.mult, [xk.r, ei.r], [rke[d][tb]])
                ee = self.tmpf.next()
                k.act(ee.t[:], pe_.t[:], AF.Exp, [pe_.r], [ee.r], scale=-1.0 / 16.0)
                k.tt("dve", kend[d][:, tb * 4:(tb + 1) * 4, :], xktv, ee.t[:, :].rearrange("p (c f) -> p c f", f=128), ALU.mult, [xkt.r, ee.r], [rkend[d][tb]])

        def state_step(S, d, c, store):
            pu = self.ps.next()
            k.mm(pu.t[:, 0:256], kend[d][:, c, :], V[:, c, :], True, True, [rkend[d][c // 4], rV[c // 4]], [pu.r])
            store()
            for h in range(2):
                hs = slice(h * 64, (h + 1) * 64)
                k.stt("dve", S.t[hs, :], S.t[hs, :], dec.t[hs, d, c:c + 1], pu.t[hs, h * 128:(h + 1) * 128], ALU.mult, ALU.add,
                      [S.r, rdec[d][c // 4], pu.r], [S.r])

        k.memset("pool", Sbk.t[:], 0.0, [Sbk.r])
        k.memset("pool", Sf.t[:], 0.0, [Sf.r])
        for c in range(31, -1, -1):
            state_step(Sbk, 1, c, lambda c=c: k.copy("act", Sb[:, c, :], Sbk.t[:], [Sbk.r], [rSb[c]]))
        for tb in range(8):
            po = [self.acc[0], self.acc[1]]
            grt = []
            for h in range(2):
                g = self.tmpf.next()
                k.dma(g.t[:], self.gr[h * 128:(h + 1) * 128, tb * 512:(tb + 1) * 512], [], [g.r], g.name)
                grt.append(g)
            for cc in range(4):
                c = tb * 4 + cc
                cs_ = slice(c * 128, (c + 1) * 128)
                sfb = Sfb.next()
                state_step(Sf, 0, c, lambda sfb=sfb: k.copy("act", sfb.t[:], Sf.t[:], [Sf.r], [sfb.r]))
                for h in range(2):
                    hs = slice(h * 64, (h + 1) * 64)
                    ats = []
                    for d in range(2):
                        pa = self.ps.next()
                        k.mm(pa.t[:, 0:128], ke[d][hs, cs_], qe[d][hs, cs_], True, True, [rke[d][tb], rqe[d][tb]], [pa.r])
                        at = self.tmpb.next()
                        k.tt("dve", at.t[:, 0:128], pa.t[:, 0:128], TR[:, 4 + d, :], ALU.mult, [pa.r, self.trit.r], [at.r])
                        ats.append(at)
                    o = po[h].t[:, cc * 128:(cc + 1) * 128]
                    Vh = V[:, c, h * 128:(h + 1) * 128]
                    k.mm(o, Vh, ats[0].t[:, 0:128], True, False, [rV[tb], ats[0].r], [po[h].r])
                    k.mm(o, Vh, ats[1].t[:, 0:128], False, False, [rV[tb], ats[1].r], [po[h].r])
                    k.mm(o, sfb.t[hs, :], qe[0][hs, cs_], False, False, [sfb.r, rqe[0][tb]], [po[h].r])
                    k.mm(o, Sb[hs, c, :], qe[1][hs, cs_], False, True, [rSb[c], rqe[1][tb]], [po[h].r])
            for h in range(2):
                sq = self.tmpb.next()
                k.act(sq.t[:], po[h].t[:], AF.Square, [po[h].r], [sq.r])
                pn = self.ps.next()
                k.mm(pn.t[:], self.ones.t[:], sq.t[:], True, True, [sq.r, self.ones.r], [pn.r])
                rs = self.tmpf.next()
                self.rstd(rs.t[:], pn.t[:], 1.0 / 128.0, 0, [pn.r], rs.r)
                y = self.tmpf.next()
                k.stt("dve", y.t[:], po[h].t[:], self.pv.t[:, 2:3], rs.t[:], ALU.mult, ALU.mult, [po[h].r, rs.r, self.pv.r], [y.r])
                sl = self.tmpf.next()
                k.act(sl.t[:], grt[h].t[:], AF.Silu, [grt[h].r], [sl.r])
                yb = self.tmpb.next()
                k.tt("dve", yb.t[:], y.t[:], sl.t[:], ALU.mult, [y.r, sl.r], [yb.r])
                k.dma(self.yT[self.ybase[1] + h * 128: self.ybase[1] + (h + 1) * 128, tb * 512:(tb + 1) * 512], yb.t[:], [yb.r], [], yb.name + "o", is_out=True)

    def mla(self):
        k = self.k
        AB, AFt = self.AB.t, self.AF.t
        cqn = lambda kc, c0, n: AB[:, kc * 4096 + c0: kc * 4096 + c0 + n]
        ckvn = lambda kc, c0, n: AB[:, 8192 + kc * 4096 + c0: 8192 + kc * 4096 + c0 + n]
        QN = AB[:, 16384:20480]; QR = AB[0:64, 20480:24576]; KN = AB[:, 24576:28672]; KRo = AB[0:64, 28672:32768]
        V = AB[:, 32768:36864].rearrange("p (c f) -> p c f", f=128)
        KR = AFt[0:64, 0:4096]
        rcq = [Res("cqn") for _ in range(8)]
        rckv = [Res("ckvn") for _ in range(8)]
        rKR = [Res("KR") for _ in range(8)]
        wq, wkv, sacc, onesf = self.mla_small()
        k.memset("pool", onesf.t[:], 1.0, writes=[onesf.r])
        for (dst, src) in ((wq, self.wuq), (wkv, self.wukv)):
            for h in range(2):
                st_ = self.tmpf.next()
                sv = st_.t[:, :].rearrange("p (a f) -> p a f", f=256)
                k.dma(sv, src[h, :, :].rearrange("(a p) f -> p a f", p=128), [], [st_.r], st_.name)
                k.copy("dve", dst.t[:, h, :, :], sv, [st_.r], [dst.r])
        for (src, dst, res, gcol) in ((self.cq, cqn, rcq, 3), (self.ckv, ckvn, rckv, 5)):
            for tb in range(8):
                xs = []
                pb = self.ps.next()
                for kc in range(2):
                    x = self.tmpf.next()
                    k.dma(x.t[:], src[kc * 128:(kc + 1) * 128, tb * 512:(tb + 1) * 512], [], [x.r], x.name)
                    sq = self.tmpb.next()
                    k.act(sq.t[:], x.t[:], AF.Square, [x.r], [sq.r])
                    k.mm(pb.t[:], self.ones.t[:], sq.t[:], kc == 0, kc == 1, [sq.r, self.ones.r], [pb.r])
                    xs.append(x)
                rs = self.tmpf.next()
                self.rstd(rs.t[:], pb.t[:], 1.0 / 256.0, 0, [pb.r], rs.r)
                for kc in range(2):
                    k.stt("dve", dst(kc, tb * 512, 512), xs[kc].t[:], self.pv.t[:, gcol + kc: gcol + kc + 1], rs.t[:], ALU.mult, ALU.mult,
                          [xs[kc].r, rs.r, self.pv.r], [res[tb]])

        def rope_mix(out, a, b, ga, gb, rs_ap, cs_t, reads, wres):
            t1 = self.tmpf.next()
            t2 = self.tmpf.next()
            if rs_ap is None:
                k.ts("dve", t1.t[0:64, :], a, ga, None, ALU.mult, None, reads, [t1.r])
                k.ts("dve", t2.t[0:64, :], b, gb, None, ALU.mult, None, reads, [t2.r])
            else:
                k.stt("dve", t1.t[0:64, :], a, ga, rs_ap, ALU.mult, ALU.mult, reads, [t1.r])
                k.stt("dve", t2.t[0:64, :], b, gb, rs_ap, ALU.mult, ALU.mult, reads, [t2.r])
            k.tt("dve", t1.t[0:64, :], t1.t[0:64, :], cs_t[0], ALU.mult, [t1.r, cs_t[2]], [t1.r])
            k.tt("pool", t2.t[0:64, :], t2.t[0:64, :], cs_t[1], ALU.mult, [t2.r, cs_t[2]], [t2.r])
            k.tt("dve", out, t1.t[0:64, :], t2.t[0:64, :], ALU.add, [t1.r, t2.r], [wres])

        def load_cs(tb):
            c1 = self.tmpf.next()
            k.dma(c1.t[0:64, :], self.cs[0:64, tb * 512:(tb + 1) * 512], [], [c1.r], c1.name)
            c2 = self.tmpf.next()
            k.dma(c2.t[0:64, :], self.cs[64:128, tb * 512:(tb + 1) * 512], [], [c2.r], c2.name)
            r = Res("cs")
            return (c1.t[0:64, :], c2.t[0:64, :], c1.r, c2.r)

        for tb in range(8):
            a = self.tmpf.next()
            self.load_kro(a, tb, False)
            b = self.tmpf.next()
            self.load_kro(b, tb, True)
            c1, c2, r1, r2 = load_cs(tb)
            t1 = self.tmpf.next()
            k.stt("dve", t1.t[0:64, :], a.t[0:64, :], self.pv.t[0:64, 11:12], c1, ALU.mult, ALU.mult, [a.r, r1, self.pv.r], [t1.r])
            t2 = self.tmpf.next()
            k.stt("dve", t2.t[0:64, :], b.t[0:64, :], self.pv.t[0:64, 12:13], c2, ALU.mult, ALU.mult, [b.r, r2, self.pv.r], [t2.r])
            k.tt("dve", KR[:, tb * 512:(tb + 1) * 512], t1.t[0:64, :], t2.t[0:64, :], ALU.add, [t1.r, t2.r], [rKR[tb]])
        for h in range(2):
            rQN = [Res("QN") for _ in range(8)]; rQR = [Res("QR") for _ in range(8)]
            rKN = [Res("KN") for _ in range(8)]; rKRo = [Res("KRo") for _ in range(8)]; rVv = [Res("V") for _ in range(8)]
            for tb in range(8):
                ts_ = slice(tb * 512, (tb + 1) * 512)
                pqn = self.ps.next(); pqr = self.ps.next(); pqs = self.ps.next()
                for (pp, m, c0) in ((pqn, 128, 0), (pqr, 64, 128), (pqs, 64, 192)):
                    for kc in range(2):
                        k.mm(pp.t[0:m, :], wq.t[:, h, kc, c0:c0 + m], cqn(kc, tb * 512, 512), kc == 0, kc == 1, [wq.r, rcq[tb]], [pp.r])
                s1 = self.tmpb.next()
                k.act(s1.t[:], pqn.t[:], AF.Square, [pqn.r], [s1.r])
                s2 = self.tmpb.next()
                k.act(s2.t[0:64, :], pqr.t[0:64, :], AF.Square, [pqr.r], [s2.r])
                pb = self.ps.next()
                k.mm(pb.t[:], self.ones.t[:], s1.t[:], True, False, [s1.r, self.ones.r], [pb.r])
                k.mm(pb.t[:], self.ones.t[0:64, :], s2.t[0:64, :], False, True, [s2.r, self.ones.r], [pb.r])
                rs = self.tmpf.next()
                self.rstd(rs.t[:], pb.t[:], 1.0, 2, [pb.r], rs.r)
                k.stt("dve", QN[:, ts_], pqn.t[:], self.pv.t[:, 7:8], rs.t[:], ALU.mult, ALU.mult, [pqn.r, rs.r, self.pv.r], [rQN[tb]])
                c1, c2, r1, r2 = load_cs(tb)
                t1 = self.tmpf.next()
                k.stt("dve", t1.t[0:64, :], pqr.t[0:64, :], self.pv.t[0:64, 8:9], rs.t[0:64, :], ALU.mult, ALU.mult, [pqr.r, rs.r, self.pv.r], [t1.r])
                k.tt("dve", t1.t[0:64, :], t1.t[0:64, :], c1, ALU.mult, [t1.r, r1], [t1.r])
                t2 = self.tmpf.next()
                k.stt("dve", t2.t[0:64, :], pqs.t[0:64, :], self.pv.t[0:64, 9:10], rs.t[0:64, :], ALU.mult, ALU.mult, [pqs.r, rs.r, self.pv.r], [t2.r])
                k.tt("pool", t2.t[0:64, :], t2.t[0:64, :], c2, ALU.mult, [t2.r, r2], [t2.r])
                k.tt("dve", QR[:, ts_], t1.t[0:64, :], t2.t[0:64, :], ALU.add, [t1.r, t2.r], [rQR[tb]])
                pkn = self.ps.next()
                for kc in range(2):
                    k.mm(pkn.t[:], wkv.t[:, h, kc, 0:128], ckvn(kc, tb * 512, 512), kc == 0, kc == 1, [wkv.r, rckv[tb]], [pkn.r])
                kr = self.tmpf.next()
                self.load_kro(kr, tb, False)
                s3 = self.tmpb.next()
                k.act(s3.t[:], pkn.t[:], AF.Square, [pkn.r], [s3.r])
                s4 = self.tmpb.next()
                k.act(s4.t[0:64, :], kr.t[0:64, :], AF.Square, [kr.r], [s4.r])
                pb2 = self.ps.next()
                k.mm(pb2.t[:], self.ones.t[:], s3.t[:], True, False, [s3.r, self.ones.r], [pb2.r])
                k.mm(pb2.t[:], self.ones.t[0:64, :], s4.t[0:64, :], False, True, [s4.r, self.ones.r], [pb2.r])
                rk_ = self.tmpf.next()
                self.rstd(rk_.t[:], pb2.t[:], 1.0 / 192.0, 0, [pb2.r], rk_.r)
                k.stt("dve", KN[:, ts_], pkn.t[:], self.pv.t[:, 10:11], rk_.t[:], ALU.mult, ALU.mult, [pkn.r, rk_.r, self.pv.r], [rKN[tb]])
                k.tt("dve", KRo[:, ts_], KR[:, ts_], rk_.t[0:64, :], ALU.mult, [rKR[tb], rk_.r], [rKRo[tb]])
                pvv = self.ps.next()
                for c in range(4):
                    for kc in range(2):
                        k.mm(pvv.t[:, c * 128:(c + 1) * 128], ckvn(kc, tb * 512 + c * 128, 128), wkv.t[:, h, kc, 128:256], kc == 0, kc == 1,
                             [wkv.r, rckv[tb]], [pvv.r])
                k.copy("act", V[:, tb * 4:(tb + 1) * 4, :], pvv.t[:, :].rearrange("p (c f) -> p c f", f=128), [pvv.r], [rVv[tb]])
            items = [(qb, kc) for qb in range(8) for kc in range(32)]

            def qk(i):
                qb, kc = items[i]
                qs = slice(qb * 512, (qb + 1) * 512)
                ks = slice(kc * 128, (kc + 1) * 128)
                ps = self.ps_att_next()
                k.mm(ps.t[:], KN[:, ks], QN[:, qs], True, False, [rKN[kc // 4], rQN[qb]], [ps.r])
                k.mm(ps.t[:], KRo[:, ks], QR[:, qs], False, True, [rKRo[kc // 4], rQR[qb]], [ps.r])
                return ps

            pending = qk(0)
            for i, (qb, kc) in enumerate(items):
                qs = slice(qb * 512, (qb + 1) * 512)
                ps = pending
                if i + 1 < len(items):
                    pending = qk(i + 1)
                po = self.acc[(qb % 2)]
                if kc == 0:
                    sa = sacc.next()
                pT = self.tmpb.next()
                k.act(pT.t[:], ps.t[:], AF.Exp, [ps.r], [pT.r])
                k.mm(po.t[:], V[:, kc, :], pT.t[:], kc == 0, kc == 31, [rVv[kc // 4], pT.r], [po.r])
                if kc == 0:
                    k.copy("dve", sa.t[:], pT.t[:], [pT.r], [sa.r])
                else:
                    k.tt("dve", sa.t[:], sa.t[:], pT.t[:], ALU.add, [sa.r, pT.r], [sa.r])
                if kc == 31:
                    pl = self.acc[2]
                    k.mm(pl.t[:], onesf.t[:], sa.t[:], True, True, [onesf.r, sa.r], [pl.r])
                    rc = self.tmpf.next()
                    k.act(rc.t[:], pl.t[:], AF.Ln, reads=[pl.r], writes=[rc.r])
                    k.act(rc.t[:], rc.t[:], AF.Exp, reads=[rc.r], writes=[rc.r], scale=-1.0)
                    y = self.tmpb.next()
                    k.tt("dve", y.t[:], po.t[:], rc.t[:], ALU.mult, [po.r, rc.r], [y.r])
                    k.dma(self.yT[self.ybase[2] + h * 128: self.ybase[2] + (h + 1) * 128, qs], y.t[:], [y.r], [], y.name + "o", is_out=True)

    def ps_att_next(self):
        if not hasattr(self, "_pa"):
            self._pa = Rot(self.ps.b[0:4])
        return self._pa.next()


def _swap64(v):
    return np.concatenate([v[..., 32:64], v[..., 0:32]], axis=-1)


def _pad128(v):
    o = np.zeros(128, np.float32)
    o[: v.shape[0]] = v
    return o


def const_tables():
    s = np.arange(128)[:, None]
    t = np.arange(128)[None, :]
    tri = np.stack([s <= t, s > t, s >= t, s < t, s <= t, s > t], axis=1).astype(np.float32)
    half = 32
    inv = (10000.0 ** (-np.arange(half, dtype=np.float32) / half)).astype(np.float32)
    ang = np.arange(SEQ, dtype=np.float32)[:, None] * inv[None, :]
    cos, sin = np.cos(ang).astype(np.float32).T, np.sin(ang).astype(np.float32).T
    cs = np.concatenate([cos, cos, -sin, sin], axis=0)
    return np.ascontiguousarray(tri), np.ascontiguousarray(cs)


def na_bias_table(rpb_heads):
    out = np.full((4, 128, 5, 5, 128), -30000.0, np.float32)
    reps = {0: 2, 1: 0, 2: 1, 3: 30, 4: 31}
    pidx = np.arange(128)
    kcol = pidx % 64
    qi = np.arange(128)
    qcol = qi % 64
    c0 = np.clip(qcol - 8, 0, 48)
    col_ok = (kcol[:, None] >= c0[None, :]) & (kcol[:, None] < c0[None, :] + 16)
    dc = np.clip(kcol[:, None] - qcol[None, :] + 15, 0, 30)
    for cls, t in reps.items():
        a0, ns, c = na_pair_info(t)
        assert c == cls
        qrow = 2 * t + qi // 64
        r0q = np.clip(qrow - 4, 0, 56)
        for sl in range(ns):
            krow = 2 * (a0 + sl) + pidx // 64
            row_ok = (krow[:, None] >= r0q[None, :]) & (krow[:, None] <= r0q[None, :] + 7)
            dr = np.clip(krow[:, None] - qrow[None, :] + 7, 0, 14)
            ok = row_ok & col_ok
            g = rpb_heads[:, dr, dc]
            out[:, :, cls, sl, :] = np.where(ok[None], g, np.float32(-30000.0))
    return np.ascontiguousarray(out.reshape(4, 128, NA_TAB))


def prep_B(zT, ztok, P, l, p, tri, cs):
    f = np.ascontiguousarray
    ones = np.ones((16, SEQ), np.float32)
    z15 = np.zeros((15, 128), np.float32)
    gsl = slice(p * 128, (p + 1) * 128)
    qn, kn = P["mla_q_norm"][l], P["mla_k_norm"][l]
    pv = np.zeros((128, 16), np.float32)
    pv[:, 0] = np.tile(P["na_q_norm"][l], 2)
    pv[:, 1] = np.tile(P["na_k_norm"][l], 2)
    pv[:, 2] = P["gla_out_norm"][l]
    pv[:, 3:5] = P["mla_cq_norm"][l].reshape(2, 128).T
    pv[:, 5:7] = P["mla_ckv_norm"][l].reshape(2, 128).T
    pv[:, 7] = qn[:128]
    pv[:, 8] = _pad128(qn[128:])
    pv[:, 9] = _pad128(_swap64(qn[128:]))
    pv[:, 10] = kn[:128]
    pv[:, 11] = _pad128(kn[128:])
    pv[:, 12] = _pad128(_swap64(kn[128:]))
    wuq, wukv = P["mla_w_uq"][l], P["mla_w_ukv"][l]
    wq = np.stack([np.concatenate([wuq[:, h * 192: h * 192 + 192], _swap64(wuq[:, h * 192 + 128: h * 192 + 192])], axis=1)
                   for h in (2 * p, 2 * p + 1)])
    wkv = np.stack([wukv[:, h * 256:(h + 1) * 256] for h in (2 * p, 2 * p + 1)])
    params = {"bcat": na_bias_table(P["na_rpb"][l][4 * p: 4 * p + 4]),
              "upf": f(np.concatenate([P["gla_gf_up"][l][:, gsl], P["gla_gf_bias"][l][None, gsl], z15], 0)),
              "upb": f(np.concatenate([P["gla_gb_up"][l][:, gsl], P["gla_gb_bias"][l][None, gsl], z15], 0)),
              "wuq": f(wq), "wukv": f(wkv), "pvec": pv}
    if zT is None:
        return params
    kr = zT[3616:3680]
    return {
        "naq": f(zT[p * 256:(p + 1) * 256]), "nak": f(zT[512 + p * 256: 512 + (p + 1) * 256]), "nav": f(ztok[:, p * 256:(p + 1) * 256]),
        "bcat": na_bias_table(P["na_rpb"][l][4 * p: 4 * p + 4]),
        "gq": f(zT[1536 + p * 128: 1536 + (p + 1) * 128]), "gk": f(zT[1792 + p * 128: 1792 + (p + 1) * 128]),
        "gkt": f(ztok[:, 512 + p * 128: 512 + (p + 1) * 128]), "gvt": f(ztok[:, 768 + p * 256: 768 + (p + 1) * 256]),
        "gr": f(zT[2560 + p * 256: 2560 + (p + 1) * 256]),
        "gff": f(np.concatenate([zT[3072:3088], ones], 0)), "gfb": f(np.concatenate([zT[3088:3104], ones], 0)),
        "upf": f(np.concatenate([P["gla_gf_up"][l][:, gsl], P["gla_gf_bias"][l][None, gsl], z15], 0)),
        "upb": f(np.concatenate([P["gla_gb_up"][l][:, gsl], P["gla_gb_bias"][l][None, gsl], z15], 0)),
        "tri": tri, "cq": f(zT[3104:3360]), "ckv": f(zT[3360:3616]),
        "kro": f(np.concatenate([kr, kr[32:64], kr[0:32]], 0)), "cs": cs,
        "wuq": f(wq), "wukv": f(wkv), "pvec": pv,
    }


class VBuf:
    def __init__(self, t, name):
        self.t = t
        self.r = Res(name)
        self.name = name


class Carve:
    def __init__(self, big, nwords):
        self.big, self.n, self.off = big, nwords, 0

    def f32(self, n):
        a = self.big[:, self.off:self.off + n]
        self.off += n
        assert self.off <= self.n, (self.off, self.n)
        return a

    def bf16(self, n):
        assert n % 2 == 0
        return self.f32(n // 2).bitcast(BF16)


BIGW = 44032 + 2048


class FProg(AProg, BProg):
    def __init__(self, L=4):
        nc = bass.Bass("TRN2", target_bir_lowering=False)
        self.nc = nc
        self.L = L
        din = lambda n, s: nc.dram_tensor(n, s, F32, kind="ExternalInput").ap()
        scr = lambda n, s: nc.dram_tensor(n, s, F32, kind="Internal").ap()
        self.xin_d = din("xin", [1024, SEQ])
        self.W = {n: din(n, [L] + sh) for n, sh in (
            ("f1w1", [1024, DFF]), ("f1w3", [1024, DFF]), ("f1w2", [DFF, 1024]),
            ("f2w1", [1024, DFF]), ("f2w3", [1024, DFF]), ("f2w2", [DFF, 1024]),
            ("win", [1024, DIN]), ("wbr", [1536, 1024]), ("wout", [1024, 1024]))}
        self.pva_d = din("pva", [L + 1, 128, 24])
        self.bcat_d = din("bcat", [L, 2, 4, 128, NA_TAB])
        self.upf_d = din("upf", [L, 2, 32, 128]); self.upb_d = din("upb", [L, 2, 32, 128])
        self.wuq_d = din("wuq", [L, 2, 2, 256, 256]); self.wukv_d = din("wukv", [L, 2, 2, 256, 256])
        self.pvb_d = din("pvb", [L, 2, 128, 16])
        self.tri = din("tri", [128, 6, 128]); self.cs = din("cs", [128, SEQ]); self.ones16 = din("ones16", [16, SEQ])
        self.xout_d = nc.dram_tensor("xout", [1024, SEQ], F32, kind="ExternalOutput").ap()
        self.XS = scr("XS", [1024, SEQ]); self.ZT = scr("ZT", [DIN, SEQ]); self.ZTOK = scr("ZTOK", [SEQ, NZT]); self.YT = nc.dram_tensor("YT", [1536, SEQ], BF16, kind="Internal").ap()
        with contextlib.ExitStack() as st:
            self.k = K(nc, st)
            self.build()
            self.k.finish()

    def load_gfl(self, g, d, tb):
        cols = slice(tb * 512, (tb + 1) * 512)
        self.k.dma(g.t[0:16, :], self.ZT[3072 + d * 16: 3088 + d * 16, cols], [], [g.r], g.name)
        self.k.dma(g.t[16:32, :], self.ones16[:, cols], [], [g.r], g.name)

    def load_kro(self, buf, tb, swapped):
        cols = slice(tb * 512, (tb + 1) * 512)
        if swapped:
            self.k.dma(buf.t[0:32, :], self.ZT[3648:3680, cols], [], [buf.r], buf.name)
            self.k.dma(buf.t[32:64, :], self.ZT[3616:3648, cols], [], [buf.r], buf.name)
        else:
            self.k.dma(buf.t[0:64, :], self.ZT[3616:3680, cols], [], [buf.r], buf.name)

    def gla_small(self):
        return self._gs

    def na_pt(self):
        return self._npt

    def mla_small(self):
        return self._ms

    def build(self):
        k = self.k
        L = self.L
        big = k.sb("BIG", [128, BIGW], F32).t
        self.tmpf = k.rot("tmpf", 8, [128, 512], F32)
        self.tmpb = k.rot("tmpb", 4, [128, 512], BF16)
        self.sq = self.tmpb
        self.ones = k.sb("ones", [128, 128], BF16)
        self.bd = k.sb("bd", [128, 128], BF16)
        self.cst = k.sb("cst", [128, 8], F32)
        self.epsb = self.cst
        self.trit = k.sb("trit", [128, 6, 128], F32)
        pvA = k.sb("pvA", [128, 24], F32)
        pvB = k.sb("pvB", [128, 16], F32)
        banks = k.psum(8)
        k.memset("pool", self.ones.t[:], 1.0, writes=[self.ones.r])
        k.memset("pool", self.bd.t[:], 0.0, writes=[self.bd.r])
        k.memset("pool", self.bd.t[0:64, 0:64], 1.0, writes=[self.bd.r])
        k.memset("pool", self.bd.t[64:128, 64:128], 1.0, writes=[self.bd.r])
        for j, v in enumerate((EPS, 64.0 * EPS, 192.0 * EPS, 1.0)):
            k.memset("pool", self.cst.t[:, j:j + 1], v, writes=[self.cst.r])
        k.dma(self.trit.t[:], self.tri[:, :, :], [], [self.trit.r], "tri")
        ca = Carve(big, BIGW)
        self.X = VBuf(ca.f32(8 * TOK).rearrange("p (a b) -> p a b", a=8), "X")
        self.arena = VBuf(ca.bf16(30720), "arena")
        self.stage = Rot([VBuf(ca.f32(WSLOT), "stg%d" % i) for i in range(2)])
        self.wb = Rot([VBuf(ca.bf16(WSLOT), "wb%d" % i) for i in range(4)])
        self.macc = Rot([VBuf(ca.f32(512), "macc%d" % i) for i in range(2)])
        self.gpool = Rot([VBuf(ca.f32(512), "gp%d" % i) for i in range(4)])
        cb = Carve(big, BIGW)
        self.AF = VBuf(cb.f32(13952), "AF")
        self.AB = VBuf(cb.bf16(36864), "AB")
        self._gs = (VBuf(cb.f32(64).rearrange("p (a b) -> p a b", a=2), "dec"),
                    VBuf(cb.f32(256)[0:32, :].rearrange("p (a b) -> p a b", a=2), "gup"),
                    VBuf(cb.bf16(256)[0:32, :].rearrange("p (a b) -> p a b", a=2), "gupb"),
                    VBuf(cb.f32(128), "Sf"), VBuf(cb.f32(128), "Sbk"),
                    Rot([VBuf(cb.bf16(128), "Sfb%d" % i) for i in range(2)]))
        self._ms = (VBuf(cb.bf16(1024).rearrange("p (a b c) -> p a b c", a=2, b=2), "wq"),
                    VBuf(cb.bf16(1024).rearrange("p (a b c) -> p a b c", a=2, b=2), "wkv"),
                    Rot([VBuf(cb.f32(512), "sacc%d" % i) for i in range(2)]), VBuf(cb.f32(128), "onesf"))
        self.banks = banks.b
        self._npt = Rot([VBuf(cb.bf16(640), "napt%d" % i) for i in range(3)])
        psA = Rot(banks.b)
        psB = Rot(banks.b[0:5])
        self._pa = Rot(banks.b[0:4])
        self.acc = banks.b[5:8]
        self.yT = self.YT
        for l in range(L + 1):
            do_k3, do_k1 = l > 0, l < L
            self.ps = psA
            self.pv = pvA
            k.dma(pvA.t[:], self.pva_d[l, :, :], [], [pvA.r], "pvA")
            for half in range(2):
                hs = slice(half * TOK, (half + 1) * TOK)
                self.Xr = [[Res("X") for _ in range(NTB)] for _ in range(8)]
                src = self.xin_d if l == 0 else self.XS
                for dc in range(8):
                    k.dma(self.X.t[:, dc, :], src[dc * 128:(dc + 1) * 128, hs], reads=[], writes=self.Xr[dc], key="xin%d" % dc)
                if do_k3:
                    self.gates = self.ZT[3680:6752, hs]
                    self.yin = self.YT[:, hs]
                    self.wbr = self.W["wbr"][l - 1]
                    self.wout = self.W["wout"][l - 1]
                    self.k3()
                    k.S.barrier()
                    self.ffn(0, self.W["f2w1"][l - 1], self.W["f2w3"][l - 1], self.W["f2w2"][l - 1])
                    k.S.barrier()
                if do_k1:
                    self.ffn(8, self.W["f1w1"][l], self.W["f1w3"][l], self.W["f1w2"][l])
                    k.S.barrier()
                    self.win = self.W["win"][l]
                    self.zT = self.ZT[:, hs]
                    self.ztok = self.ZTOK[hs, :]
                    self.zphase()
                dst = self.XS if do_k1 else self.xout_d
                for dc in range(8):
                    k.dma(dst[dc * 128:(dc + 1) * 128, hs], self.X.t[:, dc, :], reads=self.Xr[dc], writes=[], key="xo%d" % dc, is_out=True)
                k.S.barrier()
            if not do_k1:
                break
            self.ps = psB
            self.pv = pvB
            self.yT = self.YT
            for p in range(2):
                k.dma(pvB.t[:], self.pvb_d[l, p, :, :], [], [pvB.r], "pvB")
                self.ybase = (p * 256, 512 + p * 256, 1024 + p * 256)
                Z, ZK = self.ZT, self.ZTOK
                self.naq = Z[p * 256:(p + 1) * 256, :]; self.nak = Z[512 + p * 256: 512 + (p + 1) * 256, :]
                self.nav = ZK[:, p * 256:(p + 1) * 256]
                self.bcat = self.bcat_d[l, p]
                self.gq = Z[1536 + p * 128: 1536 + (p + 1) * 128, :]; self.gk = Z[1792 + p * 128: 1792 + (p + 1) * 128, :]
                self.gkt = ZK[:, 512 + p * 128: 512 + (p + 1) * 128]; self.gvt = ZK[:, 768 + p * 256: 768 + (p + 1) * 256]
                self.gr = Z[2560 + p * 256: 2560 + (p + 1) * 256, :]
                self.upf = self.upf_d[l, p]; self.upb = self.upb_d[l, p]
                self.cq = Z[3104:3360, :]; self.ckv = Z[3360:3616, :]
                self.wuq = self.wuq_d[l, p]; self.wukv = self.wukv_d[l, p]
                self.na()
                k.S.barrier()
                self.gla()
                k.S.barrier()
                self.mla()
                k.S.barrier()


def _gain_cols(g):
    return np.ascontiguousarray(g.reshape(8, 128).T)


_PROG = {}


def kernel(**inputs):
    P = {k_: np.asarray(v, dtype=np.float32) for k_, v in inputs.items()}
    x = P["x"]
    Bn, S, D = x.shape
    L = P["w_in"].shape[0]
    tri, cs = const_tables()
    pva = np.zeros((L + 1, 128, 24), np.float32)
    for l in range(L + 1):
        if l > 0:
            pva[l, :, 0:8] = _gain_cols(P["ffn2_norm"][l - 1])
        if l < L:
            pva[l, :, 8:16] = _gain_cols(P["ffn1_norm"][l])
            pva[l, :, 16:24] = _gain_cols(P["mix_norm"][l])
    bp = [[prep_B(None, None, P, l, p, tri, cs) for p in range(2)] for l in range(L)]
    stack = lambda name: np.ascontiguousarray(np.stack([np.stack([bp[l][p][name] for p in range(2)]) for l in range(L)]))
    common = {
        "f1w1": P["ffn1_w1"], "f1w3": P["ffn1_w3"], "f1w2": P["ffn1_w2"],
        "f2w1": P["ffn2_w1"], "f2w3": P["ffn2_w3"], "f2w2": P["ffn2_w2"],
        "win": P["w_in"], "wout": P["w_out"],
        "wbr": np.ascontiguousarray(np.concatenate([P["w_br_na"], P["w_br_gla"], P["w_br_mla"]], axis=1)),
        "pva": pva, "bcat": stack("bcat"), "upf": stack("upf"), "upb": stack("upb"),
        "wuq": stack("wuq"), "wukv": stack("wukv"), "pvb": stack("pvec"),
        "tri": tri, "cs": cs, "ones16": np.ones((16, SEQ), np.float32),
    }
    if L not in _PROG:
        _PROG[L] = FProg(L)
    prog = _PROG[L]
    in_maps = []
    for b in range(Bn):
        m = dict(common)
        m["xin"] = np.ascontiguousarray(x[b].T)
        in_maps.append(m)
    res = run_bass_kernel_spmd(prog.nc, in_maps, core_ids=list(range(Bn))).results
    out = np.stack([np.asarray(res[b]["xout"]).T for b in range(Bn)], axis=0)
    return np.ascontiguousarray(out.astype(np.float32))
```

```python
import contextlib
import numpy as np
import concourse.bass as bass
import concourse.mybir as mybir
from concourse.bass_utils import run_bass_kernel_spmd

F32 = mybir.dt.float32
BF16 = mybir.dt.bfloat16
AF = mybir.ActivationFunctionType
ALU = mybir.AluOpType
EPS = 1e-6
SAME_ENGINE_RAW = True


class Res:
    __slots__ = ("name", "writer", "readers")

    def __init__(self, name=""):
        self.name = name
        self.writer = None
        self.readers = []


class Op:
    __slots__ = ("eng", "fn", "deps", "needed", "is_dma", "sem", "val", "dsem", "rawdeps")

    def __init__(self, eng, fn, is_dma, dsem):
        self.eng = eng
        self.fn = fn
        self.deps = []
        self.needed = False
        self.is_dma = is_dma
        self.dsem = dsem
        self.sem = None
        self.val = None


class Sched:
    ENG = ("pe", "act", "dve", "pool", "sp")

    def __init__(self, nc):
        self.nc = nc
        self.q = {e: [] for e in self.ENG}
        self.all = []
        self.key_eng = {}

    def op(self, eng, fn, reads=(), writes=(), dma=None):
        o = Op(eng, fn, dma is not None, dma)
        if dma is not None:
            assert self.key_eng.setdefault(dma, eng) == eng, dma
        deps = []
        raw = set()
        for r in reads:
            w = r.writer
            if w is not None:
                deps.append(w)
                raw.add(id(w))
        for r in writes:
            w = r.writer
            if w is not None:
                deps.append(w)
            deps.extend(r.readers)
        seen = set()
        for d in deps:
            if id(d) in seen or d is o:
                continue
            seen.add(id(d))
            if (not d.is_dma) and d.eng == eng:
                if eng == "pe" or (eng != "pool" and (not SAME_ENGINE_RAW or id(d) not in raw)):
                    continue
            o.deps.append(d)
            d.needed = True
        for r in reads:
            r.readers.append(o)
        for r in writes:
            r.writer = o
            r.readers = []
        self.q[eng].append(o)
        self.all.append(o)
        return o

    def wait_ops(self, eng, ops):
        o = Op(eng, None, False, None)
        for d in ops:
            if d is None:
                continue
            o.deps.append(d)
            d.needed = True
        self.q[eng].append(o)
        self.all.append(o)
        return o

    def barrier(self):
        lasts = []
        for e in self.ENG:
            for o in reversed(self.q[e]):
                if o.fn is not None and not o.is_dma:
                    lasts.append(o)
                    break
        lastdma = {}
        for o in self.all:
            if o.is_dma:
                lastdma[o.dsem] = o
        deps = lasts + list(lastdma.values())
        for e in self.ENG:
            if self.q[e]:
                self.wait_ops(e, [d for d in deps if d.is_dma or d.eng != e])

    def finalize(self):
        nc = self.nc
        with contextlib.ExitStack() as st:
            esem = {e: st.enter_context(nc.semaphore("s_" + e)) for e in self.ENG}
            dkeys = []
            for o in self.all:
                if o.is_dma and o.dsem not in dkeys:
                    dkeys.append(o.dsem)
            dsem = {k: st.enter_context(nc.semaphore("d_%s" % (k,))) for k in dkeys}
            cnt = {e: 0 for e in self.ENG}
            dcnt = {k: 0 for k in dkeys}
            for e in self.ENG:
                for o in self.q[e]:
                    if o.is_dma:
                        dcnt[o.dsem] += 16
                        o.sem = dsem[o.dsem]
                        o.val = dcnt[o.dsem]
                    elif o.needed:
                        cnt[e] += 1
                        o.sem = esem[e]
                        o.val = cnt[e]
            block = st.enter_context(nc.Block())
            engobj = {"pe": "tensor", "act": "scalar", "dve": "vector", "pool": "gpsimd", "sp": "sync"}

            def make(e):
                ops = self.q[e]

                def body(eng):
                    waited = {}
                    for o in ops:
                        for d in o.deps:
                            key = id(d.sem)
                            if waited.get(key, 0) >= d.val:
                                continue
                            waited[key] = d.val
                            eng.wait_ge(d.sem, d.val)
                        if o.fn is None:
                            continue
                        ins = o.fn(eng)
                        if o.is_dma:
                            ins.then_inc(o.sem, 16)
                        elif o.needed:
                            ins.then_inc(o.sem, 1)
                return body

            for e in self.ENG:
                if self.q[e]:
                    getattr(block, engobj[e])(make(e))


class Buf:
    def __init__(self, t, name):
        self.t = t
        self.r = Res(name)
        self.name = name


class Rot:
    def __init__(self, bufs):
        self.b = bufs
        self.i = 0

    def next(self):
        b = self.b[self.i % len(self.b)]
        self.i += 1
        return b


class K:
    def __init__(self, nc, st):
        self.nc = nc
        self.st = st
        self.S = Sched(nc)
        self.ci = 0
        self.outs = []

    def sb(self, name, shape, dt):
        return Buf(self.st.enter_context(self.nc.sbuf_tensor(name, shape, dt)), name)

    def rot(self, name, n, shape, dt):
        return Rot([self.sb("%s%d" % (name, i), shape, dt) for i in range(n)])

    def psum(self, n=8):
        self.dbl = []
        bufs = []
        for i in range(n // 2):
            t = self.st.enter_context(self.nc.psum_tensor("psd%d" % i, [128, 1024], F32))
            b0 = Buf(t[:, 0:512], "ps%d" % (2 * i))
            b1 = Buf(t[:, 512:1024], "ps%d" % (2 * i + 1))
            bufs += [b0, b1]
            self.dbl.append((t, b0.r, b1.r))
        return Rot(bufs)

    def dma(self, out, in_, reads, writes, key, eng="sp", is_out=False):
        o = self.S.op(eng, lambda e: e.dma_start(out=out, in_=in_), reads=reads, writes=writes, dma=key)
        if is_out:
            self.outs.append(o)
        return o

    def mm(self, out, lhsT, rhs, start, stop, reads, writes):
        return self.S.op("pe", lambda e: e.matmul(out, lhsT=lhsT, rhs=rhs, start=start, stop=stop),
                         reads=reads, writes=writes)

    def act(self, out, in_, func, reads, writes, scale=1.0, bias=0.0):
        return self.S.op("act", lambda e: e.activation(out=out, in_=in_, func=func, bias=bias, scale=scale),
                         reads=reads, writes=writes)

    def tt(self, eng, out, in0, in1, op, reads, writes):
        return self.S.op(eng, lambda e: e.tensor_tensor(out=out, in0=in0, in1=in1, op=op), reads=reads, writes=writes)

    def ts(self, eng, out, in0, s1, s2, op0, op1, reads, writes):
        if s2 is None:
            return self.S.op(eng, lambda e: e.tensor_single_scalar(out=out, in_=in0, scalar=s1, op=op0),
                             reads=reads, writes=writes)
        return self.S.op(eng, lambda e: e.tensor_scalar(out=out, in0=in0, scalar1=s1, scalar2=s2, op0=op0, op1=op1),
                         reads=reads, writes=writes)

    def stt(self, eng, out, in0, scalar, in1, op0, op1, reads, writes):
        return self.S.op(eng, lambda e: e.scalar_tensor_tensor(out=out, in0=in0, scalar=scalar, in1=in1, op0=op0, op1=op1),
                         reads=reads, writes=writes)

    def copy(self, eng, out, in_, reads, writes):
        if eng == "act":
            return self.S.op("act", lambda e: e.copy(out=out, in_=in_), reads=reads, writes=writes)
        return self.S.op(eng, lambda e: e.tensor_copy(out=out, in_=in_), reads=reads, writes=writes)

    def memset(self, eng, ap, val, writes):
        return self.S.op(eng, lambda e: e.memset(ap, val), writes=writes)

    def cast_eng(self, engs=("pool", "dve", "pool", "act")):
        e = engs[self.ci % len(engs)]
        self.ci += 1
        return e

    def finish(self):
        self.S.wait_ops("sp", self.outs)
        self.S.finalize()


TOK = 2048
NTB = 4
DFF = 2816
DIN = 6752
ZT_COLS = [(1024, 512), (1792, 256), (2048, 512)]
NZT = 1280
WSLOT = 2816


class AProg:
    def __init__(self, do_k3, do_k1):
        self.do_k3, self.do_k1 = do_k3, do_k1
        nc = bass.Bass("TRN2", target_bir_lowering=False)
        self.nc = nc
        din = lambda n, s: nc.dram_tensor(n, s, F32, kind="ExternalInput").ap()
        dout = lambda n, s: nc.dram_tensor(n, s, F32, kind="ExternalOutput").ap()
        self.xin = din("xin", [1024, TOK])
        self.pvec = din("pvec", [128, 24])
        if do_k3:
            self.gates = din("gates", [3072, TOK])
            self.yin = nc.dram_tensor("yin", [1536, TOK], BF16, kind="ExternalInput").ap()
            self.wbr = din("wbr", [1536, 1024])
            self.wout = din("wout", [1024, 1024])
            self.f2 = [din("f2w1", [1024, DFF]), din("f2w3", [1024, DFF]), din("f2w2", [DFF, 1024])]
        if do_k1:
            self.f1 = [din("f1w1", [1024, DFF]), din("f1w3", [1024, DFF]), din("f1w2", [DFF, 1024])]
            self.win = din("win", [1024, DIN])
            self.zT = dout("zT", [DIN, TOK])
            self.ztok = dout("ztok", [TOK, NZT])
        self.xout = dout("xout", [1024, TOK])
        with contextlib.ExitStack() as st:
            self.k = K(nc, st)
            self.build()
            self.k.finish()

    def load_w(self, view, kc, ncols):
        k = self.k
        n = kc * ncols
        assert n <= WSLOT
        stg = self.stage.next()
        wb = self.wb.next()
        sv = stg.t[:, 0:n].rearrange("p (a b) -> p a b", a=kc)
        k.dma(sv, view, reads=[], writes=[stg.r], key=stg.name)
        k.copy(k.cast_eng(), wb.t[:, 0:n], stg.t[:, 0:n], reads=[stg.r], writes=[wb.r])
        return wb.t[:, 0:n].rearrange("p (a b) -> p a b", a=kc), wb.r

    def rmsnorm(self, gcol, hview, hres, tbs):
        k = self.k
        for j, tb in enumerate(tbs):
            pb = self.ps.next()
            for kc in range(8):
                sq = self.sq.next()
                k.act(sq.t[:], self.X.t[:, kc, tb * 512:(tb + 1) * 512], AF.Square, reads=[self.Xr[kc][tb]], writes=[sq.r])
                k.mm(pb.t[:], self.ones.t[:], sq.t[:], kc == 0, kc == 7, reads=[sq.r, self.ones.r], writes=[pb.r])
            rs = self.tmpf.next()
            k.act(rs.t[:], pb.t[:], AF.Ln, reads=[pb.r, self.epsb.r], writes=[rs.r], scale=1.0 / 1024.0, bias=self.epsb.t[:, 0:1])
            k.act(rs.t[:], rs.t[:], AF.Exp, reads=[rs.r], writes=[rs.r], scale=-0.5)
            for kc in range(8):
                k.stt("dve", hview(kc, j), self.X.t[:, kc, tb * 512:(tb + 1) * 512], self.pv.t[:, gcol + kc:gcol + kc + 1],
                      rs.t[:], ALU.mult, ALU.mult, reads=[self.Xr[kc][tb], rs.r, self.pv.r], writes=[hres[kc][j]])

    def ffn(self, gcol, w1, w3, w2):
        k = self.k
        A = self.arena.t
        hres = [[Res("h") for _ in range(2)] for _ in range(8)]
        gres = [[Res("g") for _ in range(2)] for _ in range(22)]
        hv = lambda kc, t: A[:, kc * 1024 + t * 512: kc * 1024 + (t + 1) * 512]
        gv = lambda fc, t: A[:, 8192 + fc * 1024 + t * 512: 8192 + fc * 1024 + (t + 1) * 512]
        for half in range(2):
            self.rmsnorm(gcol, hv, hres, [half * 2, half * 2 + 1])
            for fp in range(11):
                w1b, r1 = self.load_w(w1[:, fp * 256:(fp + 1) * 256].rearrange("(a p) n -> p a n", p=128), 8, 256)
                w3b, r3 = self.load_w(w3[:, fp * 256:(fp + 1) * 256].rearrange("(a p) n -> p a n", p=128), 8, 256)
                for sub in range(2):
                    fc = fp * 2 + sub
                    for t in range(2):
                        pu = self.ps.next()
                        for kc in range(8):
                            k.mm(pu.t[:], w1b[:, kc, sub * 128:(sub + 1) * 128], hv(kc, t), kc == 0, kc == 7,
                                 reads=[r1, hres[kc][t]], writes=[pu.r])
                        pv = self.ps.next()
                        for kc in range(8):
                            k.mm(pv.t[:], w3b[:, kc, sub * 128:(sub + 1) * 128], hv(kc, t), kc == 0, kc == 7,
                                 reads=[r3, hres[kc][t]], writes=[pv.r])
                        s = self.tmpf.next()
                        k.act(s.t[:], pu.t[:], AF.Silu, reads=[pu.r], writes=[s.r])
                        k.tt("dve", gv(fc, t), s.t[:], pv.t[:], ALU.mult, reads=[s.r, pv.r], writes=[gres[fc][t]])
            for dc in range(8):
                w2b, r2 = self.load_w(w2[:, dc * 128:(dc + 1) * 128].rearrange("(a p) n -> p a n", p=128), 22, 128)
                for t in range(2):
                    tb = half * 2 + t
                    py = self.ps.next()
                    for fc in range(22):
                        k.mm(py.t[:], w2b[:, fc, :], gv(fc, t), fc == 0, fc == 21, reads=[r2, gres[fc][t]], writes=[py.r])
                    xs = self.X.t[:, dc, tb * 512:(tb + 1) * 512]
                    k.stt("dve", xs, py.t[:], 0.5, xs, ALU.mult, ALU.add, reads=[py.r, self.Xr[dc][tb]], writes=[self.Xr[dc][tb]])

    def k3(self):
        k = self.k
        A = self.arena.t
        ybf = lambda c, tb: A[:, c * 2048 + tb * 512: c * 2048 + (tb + 1) * 512]
        mbf = lambda c: A[:, 24576 + c * 512: 24576 + (c + 1) * 512]
        yres = [Res("y") for _ in range(12)]
        mres = [Res("m") for _ in range(8)]
        for c in range(12):
            k.dma(A[:, c * 2048:(c + 1) * 2048], self.yin[c * 128:(c + 1) * 128, :], reads=[], writes=[yres[c]], key="yin%d" % c)
        for tb in range(NTB):
            ts_ = slice(tb * 512, (tb + 1) * 512)
            for dc in range(8):
                W, rw = self.load_w(self.wbr[:, dc * 128:(dc + 1) * 128].rearrange("(a p) n -> p a n", p=128), 12, 128)
                macc = self.macc.next()
                for i in range(3):
                    gs = self.gpool.next()
                    k.dma(gs.t[:], self.gates[i * 1024 + dc * 128: i * 1024 + (dc + 1) * 128, ts_], reads=[], writes=[gs.r], key=gs.name)
                    ps = self.ps.next()
                    for c in range(4):
                        k.mm(ps.t[:], W[:, i * 4 + c, :], ybf(i * 4 + c, tb), c == 0, c == 3, reads=[rw, yres[i * 4 + c]], writes=[ps.r])
                    sg = self.tmpf.next()
                    k.act(sg.t[:], gs.t[:], AF.Sigmoid, reads=[gs.r], writes=[sg.r])
                    if i == 0:
                        k.tt("dve", macc.t[:], sg.t[:], ps.t[:], ALU.mult, reads=[sg.r, ps.r], writes=[macc.r])
                    else:
                        k.tt("dve", sg.t[:], sg.t[:], ps.t[:], ALU.mult, reads=[sg.r, ps.r], writes=[sg.r])
                        if i == 1:
                            k.tt("dve", macc.t[:], macc.t[:], sg.t[:], ALU.add, reads=[macc.r, sg.r], writes=[macc.r])
                        else:
                            k.tt("dve", mbf(dc), macc.t[:], sg.t[:], ALU.add, reads=[macc.r, sg.r], writes=[mres[dc]])
            for dc in range(8):
                W, rw = self.load_w(self.wout[:, dc * 128:(dc + 1) * 128].rearrange("(a p) n -> p a n", p=128), 8, 128)
                ps = self.ps.next()
                for c in range(8):
                    k.mm(ps.t[:], W[:, c, :], mbf(c), c == 0, c == 7, reads=[rw, mres[c]], writes=[ps.r])
                xs = self.X.t[:, dc, ts_]
                k.tt("dve", xs, ps.t[:], xs, ALU.add, reads=[ps.r, self.Xr[dc][tb]], writes=[self.Xr[dc][tb]])

    def zphase(self):
        k = self.k
        A = self.arena.t
        hres = [[Res("h2") for _ in range(4)] for _ in range(8)]
        hv = lambda kc, tb: A[:, kc * 2048 + tb * 512: kc * 2048 + (tb + 1) * 512]
        self.rmsnorm(16, hv, hres, [0, 1, 2, 3])
        ev = 0
        for fp in range(27):
            if fp in (4, 5, 8, 9):
                continue
            c0 = fp * 256
            ncols = min(256, DIN - c0)
            W, rw = self.load_w(self.win[:, c0:c0 + ncols].rearrange("(a p) n -> p a n", p=128), 8, ncols)
            for sub in range((ncols + 127) // 128):
                m = min(128, ncols - sub * 128)
                for tb in range(NTB):
                    ps = self.ps.next()
                    for kc in range(8):
                        k.mm(ps.t[0:m, :], W[:, kc, sub * 128: sub * 128 + m], hv(kc, tb), kc == 0, kc == 7,
                             reads=[rw, hres[kc][tb]], writes=[ps.r])
                    zo = self.tmpf.next()
                    k.copy(("act", "dve")[ev % 2], zo.t[0:m, :], ps.t[0:m, :], reads=[ps.r], writes=[zo.r])
                    ev += 1
                    f0 = c0 + sub * 128
                    k.dma(self.zT[f0:f0 + m, tb * 512:(tb + 1) * 512], zo.t[0:m, :], reads=[zo.r], writes=[], key=zo.name + "o", eng="act", is_out=True)
        off = 0
        for (c0, n) in ZT_COLS:
            for b in range(n // 256):
                W, rw = self.load_w(self.win[:, c0 + b * 256: c0 + (b + 1) * 256].rearrange("(a p) n -> p a n", p=128), 8, 256)
                for tc in range(16):
                    ps = self.ps.next()
                    for kc in range(8):
                        k.mm(ps.t[:, 0:256], A[:, kc * 2048 + tc * 128: kc * 2048 + (tc + 1) * 128], W[:, kc, :], kc == 0, kc == 7,
                             reads=[rw, hres[kc][tc // 4]], writes=[ps.r])
                    zo = self.tmpf.next()
                    k.copy(("act", "dve")[ev % 2], zo.t[:, 0:256], ps.t[:, 0:256], reads=[ps.r], writes=[zo.r])
                    ev += 1
                    k.dma(self.ztok[tc * 128:(tc + 1) * 128, off:off + 256], zo.t[:, 0:256], reads=[zo.r], writes=[], key=zo.name + "o", eng="act", is_out=True)
                off += 256

    def build(self):
        k = self.k
        self.X = k.sb("X", [128, 8, TOK], F32)
        self.Xr = [[Res("X") for _ in range(NTB)] for _ in range(8)]
        self.arena = k.sb("arena", [128, 30720], BF16)
        self.stage = k.rot("stg", 3, [128, WSLOT], F32)
        self.wb = k.rot("wb", 4, [128, WSLOT], BF16)
        self.tmpf = k.rot("tmpf", 6, [128, 512], F32)
        self.macc = k.rot("macc", 2, [128, 512], F32)
        self.gpool = k.rot("gp", 4, [128, 512], F32)
        self.sq = k.rot("sq", 2, [128, 512], BF16)
        self.ones = k.sb("ones", [128, 128], BF16)
        self.pv = k.sb("pv", [128, 24], F32)
        self.epsb = k.sb("epsb", [128, 1], F32)
        k.memset("pool", self.epsb.t[:], EPS, writes=[self.epsb.r])
        self.ps = k.psum(8)
        k.memset("pool", self.ones.t[:], 1.0, writes=[self.ones.r])
        k.dma(self.pv.t[:], self.pvec[:, :], reads=[], writes=[self.pv.r], key="pv")
        for dc in range(8):
            k.dma(self.X.t[:, dc, :], self.xin[dc * 128:(dc + 1) * 128, :], reads=[], writes=self.Xr[dc], key="xin%d" % dc)
        if self.do_k3:
            self.k3()
            k.S.barrier()
            self.ffn(0, *self.f2)
            k.S.barrier()
        if self.do_k1:
            self.ffn(8, *self.f1)
            k.S.barrier()
            self.zphase()
        for dc in range(8):
            k.dma(self.xout[dc * 128:(dc + 1) * 128, :], self.X.t[:, dc, :], reads=self.Xr[dc], writes=[], key="xo%d" % dc, is_out=True)


SEQ = 4096
NQB = 8
NA_CLS_W = 320
NA_TAB = 5 * 640


def na_pair_info(t):
    if t < 2:
        return 0, 4, 1 + t
    if t >= 30:
        return 28, 4, 3 + (t - 30)
    return t - 2, 5, 0


class BProg:
    def __init__(self, parts=("na", "gla", "mla")):
        nc = bass.Bass("TRN2", target_bir_lowering=False)
        self.nc = nc
        self.parts = parts
        din = lambda n, s: nc.dram_tensor(n, s, F32, kind="ExternalInput").ap()
        self.naq = din("naq", [256, SEQ]); self.nak = din("nak", [256, SEQ]); self.nav = din("nav", [SEQ, 256])
        self.bcat = din("bcat", [4, 128, NA_TAB])
        self.gq = din("gq", [128, SEQ]); self.gk = din("gk", [128, SEQ]); self.gkt = din("gkt", [SEQ, 128])
        self.gvt = din("gvt", [SEQ, 256]); self.gr = din("gr", [256, SEQ])
        self.gff = din("gff", [32, SEQ]); self.gfb = din("gfb", [32, SEQ])
        self.upf = din("upf", [32, 128]); self.upb = din("upb", [32, 128])
        self.tri = din("tri", [128, 6, 128])
        self.cq = din("cq", [256, SEQ]); self.ckv = din("ckv", [256, SEQ]); self.kro = din("kro", [128, SEQ])
        self.cs = din("cs", [128, SEQ])
        self.wuq = din("wuq", [2, 256, 256]); self.wukv = din("wukv", [2, 256, 256])
        self.pvec = din("pvec", [128, 16])
        self.yT = nc.dram_tensor("yT", [768, SEQ], BF16, kind="ExternalOutput").ap()
        with contextlib.ExitStack() as st:
            self.k = K(nc, st)
            self.build()
            self.k.finish()

    def build(self):
        k = self.k
        self.AF = k.sb("AF", [128, 13952], F32)
        self.AB = k.sb("AB", [128, 36864], BF16)
        self.tmpf = k.rot("tmpf", 8, [128, 512], F32)
        self.tmpb = k.rot("tmpb", 4, [128, 512], BF16)
        self.ones = k.sb("ones", [128, 128], BF16)
        self.bd = k.sb("bd", [128, 128], BF16)
        self.pv = k.sb("pv", [128, 16], F32)
        self.cst = k.sb("cst", [128, 8], F32)
        self.trit = k.sb("trit", [128, 6, 128], F32)
        banks = k.psum(8)
        self.banks = banks.b
        self.ps = Rot(banks.b[0:5])
        self.acc = banks.b[5:8]
        k.memset("pool", self.ones.t[:], 1.0, writes=[self.ones.r])
        k.memset("pool", self.bd.t[:], 0.0, writes=[self.bd.r])
        k.memset("pool", self.bd.t[0:64, 0:64], 1.0, writes=[self.bd.r])
        k.memset("pool", self.bd.t[64:128, 64:128], 1.0, writes=[self.bd.r])
        for j, v in enumerate((EPS, 64.0 * EPS, 192.0 * EPS, 1.0)):
            k.memset("pool", self.cst.t[:, j:j + 1], v, writes=[self.cst.r])
        k.dma(self.pv.t[:], self.pvec[:, :], [], [self.pv.r], "pv")
        k.dma(self.trit.t[:], self.tri[:, :, :], [], [self.trit.r], "tri")
        if "na" in self.parts:
            self.na()
            k.S.barrier()
        if "gla" in self.parts:
            self.gla()
            k.S.barrier()
        if "mla" in self.parts:
            self.mla()

    ybase = (0, 256, 512)

    def load_gfl(self, g, d, tb):
        src = (self.gff, self.gfb)[d]
        self.k.dma(g.t[0:32, :], src[:, tb * 512:(tb + 1) * 512], [], [g.r], g.name)

    def load_kro(self, buf, tb, swapped):
        r0 = 64 if swapped else 0
        self.k.dma(buf.t[0:64, :], self.kro[r0:r0 + 64, tb * 512:(tb + 1) * 512], [], [buf.r], buf.name)

    def na_pt(self):
        if not hasattr(self, "_npt"):
            self._npt = self.k.rot("napt", 3, [128, 640], BF16)
        return self._npt

    def gla_small(self):
        if not hasattr(self, "_gs"):
            k = self.k
            self._gs = (k.sb("dec", [128, 2, 32], F32), k.sb("gup", [32, 2, 128], F32), k.sb("gupb", [32, 2, 128], BF16),
                        k.sb("Sf", [128, 128], F32), k.sb("Sbk", [128, 128], F32), k.rot("Sfb", 2, [128, 128], BF16))
        return self._gs

    def mla_small(self):
        if not hasattr(self, "_ms"):
            self._ms = (self.k.sb("wq", [128, 2, 2, 256], BF16), self.k.sb("wkv", [128, 2, 2, 256], BF16),
                        self.k.rot("sacc", 2, [128, 512], F32), self.k.sb("onesf", [128, 128], F32))
        return self._ms

    def rstd(self, out, ps_ap, scale, cst_col, reads, wres, np_=128):
        k = self.k
        k.act(out, ps_ap, AF.Ln, reads=reads + [self.cst.r], writes=[wres], scale=scale, bias=self.cst.t[0:np_, cst_col:cst_col + 1])
        k.act(out, out, AF.Exp, reads=[wres], writes=[wres], scale=-0.5)

    def na(self):
        k = self.k
        AB, AFt = self.AB.t, self.AF.t
        qn = lambda sl, c0, n: AB[sl, c0:c0 + n]
        kn = lambda sl, c0, n: AB[sl, 4096 + c0: 4096 + c0 + n]
        V = AB[:, 8192:16384].rearrange("p (c f) -> p c f", f=256)
        rV = Res("naV")
        for c4 in range(8):
            for hh in range(2):
                st_ = self.tmpf.next()
                sv = st_.t[:, :].rearrange("p (c f) -> p c f", f=128)
                k.dma(sv, self.nav[c4 * 512:(c4 + 1) * 512, hh * 128:(hh + 1) * 128].rearrange("(c p) f -> p c f", p=128), [], [st_.r], st_.name)
                k.copy(k.cast_eng(("pool", "dve")), V[:, c4 * 4:(c4 + 1) * 4, hh * 128:(hh + 1) * 128], sv, [st_.r], [rV])
        for hp in range(2):
            rq = [Res("naq") for _ in range(NQB)]
            rk = [Res("nak") for _ in range(NQB)]
            for (src, dst, res, gcol, scale, ccol) in ((self.naq, qn, rq, 0, 1.0, 1), (self.nak, kn, rk, 1, 1.0 / 64.0, 0)):
                for tb in range(NQB):
                    x = self.tmpf.next()
                    k.dma(x.t[:], src[hp * 128:(hp + 1) * 128, tb * 512:(tb + 1) * 512], [], [x.r], x.name)
                    sq = self.tmpb.next()
                    k.act(sq.t[:], x.t[:], AF.Square, [x.r], [sq.r])
                    pb = self.ps.next()
                    k.mm(pb.t[:], self.bd.t[:], sq.t[:], True, True, [sq.r, self.bd.r], [pb.r])
                    rs = self.tmpf.next()
                    self.rstd(rs.t[:], pb.t[:], scale, ccol, [pb.r], rs.r)
                    k.stt("dve", dst(slice(0, 128), tb * 512, 512), x.t[:], self.pv.t[:, gcol:gcol + 1], rs.t[:], ALU.mult, ALU.mult,
                          [x.r, rs.r, self.pv.r], [res[tb]])
            if not hasattr(self, "bcres"):
                self.bcres = [Res("bc0"), Res("bc1")]
                self.yres = [Res("Y0"), Res("Y1")]
            dq = Rot([0, 1])
            hd = []
            for hl in range(2):
                h = hp * 2 + hl
                bct = AFt[:, hl * NA_TAB:(hl + 1) * NA_TAB]
                k.dma(bct, self.bcat[h, :, :], [], [self.bcres[hl]], "bcat%d" % hl)
                Y = AFt[0:64, 2 * NA_TAB + hl * 2048: 2 * NA_TAB + (hl + 1) * 2048].bitcast(BF16)
                hd.append((h, slice(hl * 64, (hl + 1) * 64), bct, self.bcres[hl], Y, self.yres[hl], [self.banks[4 + 2 * hl], self.banks[5 + 2 * hl]]))
            units = [(t, hl) for t in range(32) for hl in range(2)]
            npt = self.na_pt()

            def qk(u):
                t, hl = units[u]
                h, hs, bct, bcr, Y, yr, obs = hd[hl]
                a0, ns, cls = na_pair_info(t)
                d = k.dbl[dq.next()]
                for s_ in range(ns):
                    k.mm(d[0][:, s_ * 128:(s_ + 1) * 128], kn(hs, (a0 + s_) * 128, 128), qn(hs, t * 128, 128), True, True,
                         [rk[(a0 + s_) // 4], rq[t // 4]], [d[1], d[2]])
                return d

            DEPTH = 1
            pend = [qk(u) for u in range(DEPTH)]
            for u, (t, hl) in enumerate(units):
                h, hs, bct, bcr, Y, yr, obs = hd[hl]
                t2, j = divmod(t, 2)
                ob = obs[t2 % 2]
                a0, ns, cls = na_pair_info(t)
                n = ns * 128
                d = pend.pop(0)
                if u + DEPTH < len(units):
                    pend.append(qk(u + DEPTH))
                k.tt("dve", d[0][:, 0:n], d[0][:, 0:n], bct[:, cls * 640: cls * 640 + n], ALU.add, [d[1], d[2], bcr], [d[1], d[2]])
                pT = npt.next()
                k.act(pT.t[:, 0:n], d[0][:, 0:n], AF.Exp, [d[1], d[2]], [pT.r])
                for s_ in range(ns):
                    k.mm(ob.t[0:64, j * 256: j * 256 + 128], V[:, a0 + s_, h * 64:(h + 1) * 64], pT.t[:, s_ * 128:(s_ + 1) * 128],
                         s_ == 0, s_ == ns - 1, [rV, pT.r], [ob.r])
                for s_ in range(ns):
                    k.mm(ob.t[0:64, j * 256 + 128: j * 256 + 256], self.ones.t[:, 0:64], pT.t[:, s_ * 128:(s_ + 1) * 128],
                         s_ == 0, s_ == ns - 1, [self.ones.r, pT.r], [ob.r])
                if j == 1:
                    ov = ob.t[0:64, :].rearrange("p (j t q) -> p j t q", j=2, t=2)
                    rc = self.tmpf.next()
                    rcv = rc.t[0:64, 0:256].rearrange("p (j q) -> p j q", j=2)
                    k.S.op("dve", lambda e, o=rcv, i=ov[:, :, 1, :]: e.reciprocal(out=o, in_=i), reads=[ob.r], writes=[rc.r])
                    k.tt("dve", Y[:, t2 * 256:(t2 + 1) * 256].rearrange("p (j q) -> p j q", j=2), ov[:, :, 0, :], rcv, ALU.mult,
                         [ob.r, rc.r], [yr])
            for hl in range(2):
                h, hs, bct, bcr, Y, yr, obs = hd[hl]
                k.dma(self.yT[self.ybase[0] + h * 64: self.ybase[0] + (h + 1) * 64, :], Y, [yr], [], "yna%d" % hl, is_out=True)

    def gla(self):
        k = self.k
        AB, AFt = self.AB.t, self.AF.t
        TR = self.trit.t
        sp = [AFt[:, d * 4096:(d + 1) * 4096].rearrange("p (c f) -> p c f", f=128) for d in range(2)]
        rsp = [[Res("sp") for _ in range(8)] for _ in range(2)]
        qe = [AB[:, d * 4096:(d + 1) * 4096] for d in range(2)]
        ke = [AB[:, 8192 + d * 4096: 8192 + (d + 1) * 4096] for d in range(2)]
        kend = [AB[:, 16384 + d * 4096: 16384 + (d + 1) * 4096].rearrange("p (c f) -> p c f", f=128) for d in range(2)]
        V = AB[:, 24576:32768].rearrange("p (c f) -> p c f", f=256)
        Sb = AB[:, 32768:36864].rearrange("p (c f) -> p c f", f=128)
        rqe = [[Res("qe") for _ in range(8)] for _ in range(2)]
        rke = [[Res("ke") for _ in range(8)] for _ in range(2)]
        rkend = [[Res("kend") for _ in range(8)] for _ in range(2)]
        rV = [Res("gV") for _ in range(8)]
        rSb = [Res("Sb") for _ in range(32)]
        dec, up, upb_, Sf, Sbk, Sfb = self.gla_small()
        rdec = [[Res("dec") for _ in range(8)] for _ in range(2)]
        k.dma(up.t[:, 0, :], self.upf[:, :], [], [up.r], "gup")
        k.dma(up.t[:, 1, :], self.upb[:, :], [], [up.r], "gup")
        k.copy("dve", upb_.t[:], up.t[:], [up.r], [upb_.r])
        for c4 in range(8):
            for hh in range(2):
                st_ = self.tmpf.next()
                sv = st_.t[:, :].rearrange("p (c f) -> p c f", f=128)
                k.dma(sv, self.gvt[c4 * 512:(c4 + 1) * 512, hh * 128:(hh + 1) * 128].rearrange("(c p) f -> p c f", p=128), [], [st_.r], st_.name)
                k.copy(k.cast_eng(("pool", "dve")), V[:, c4 * 4:(c4 + 1) * 4, hh * 128:(hh + 1) * 128], sv, [st_.r], [rV[c4]])
        for d in range(2):
            for tb in range(8):
                g = self.tmpf.next()
                self.load_gfl(g, d, tb)
                gb = self.tmpb.next()
                k.copy("dve", gb.t[0:32, :], g.t[0:32, :], [g.r], [gb.r])
                ps = self.ps.next()
                for c in range(4):
                    k.mm(ps.t[:, c * 128:(c + 1) * 128], gb.t[0:32, c * 128:(c + 1) * 128], upb_.t[:, d, :], True, True, [gb.r, upb_.r], [ps.r])
                e1 = self.tmpf.next()
                k.act(e1.t[:], ps.t[:], AF.Exp, [ps.r], [e1.r], scale=-1.0)
                k.act(sp[d][:, tb * 4:(tb + 1) * 4, :], e1.t[:].rearrange("p (c f) -> p c f", f=128), AF.Ln, [e1.r, self.cst.r], [rsp[d][tb]],
                      bias=self.cst.t[:, 3:4])
        for tb in range(8):
            xq = self.tmpf.next()
            k.dma(xq.t[:], self.gq[:, tb * 512:(tb + 1) * 512], [], [xq.r], xq.name)
            xk = self.tmpf.next()
            k.dma(xk.t[:], self.gk[:, tb * 512:(tb + 1) * 512], [], [xk.r], xk.name)
            xkt = self.tmpf.next()
            xktv = xkt.t[:, :].rearrange("p (c f) -> p c f", f=128)
            k.dma(xktv, self.gkt[tb * 512:(tb + 1) * 512, :].rearrange("(c p) f -> p c f", p=128), [], [xkt.r], xkt.name)
            for d in range(2):
                pb = self.ps.next()
                pe_ = self.ps.next()
                for c in range(4):
                    k.mm(pb.t[:, c * 128:(c + 1) * 128], sp[d][:, tb * 4 + c, :], TR[:, 2 * d, :], True, True, [rsp[d][tb], self.trit.r], [pb.r])
                    k.mm(pe_.t[:, c * 128:(c + 1) * 128], TR[:, 2 * d + 1, :], sp[d][:, tb * 4 + c, :], True, True, [rsp[d][tb], self.trit.r], [pe_.r])
                eb = self.tmpf.next()
                k.act(eb.t[:], pb.t[:], AF.Exp, [pb.r], [eb.r], scale=-1.0 / 16.0)
                col = 127 if d == 0 else 0
                k.copy("pool", dec.t[:, d, tb * 4:(tb + 1) * 4], eb.t[:, :].rearrange("p (c t) -> p c t", t=128)[:, :, col], [eb.r], [rdec[d][tb]])
                k.stt("dve", qe[d][:, tb * 512:(tb + 1) * 512], xq.t[:], 0.125, eb.t[:], ALU.mult, ALU.mult, [xq.r, eb.r], [rqe[d][tb]])
                ei = self.tmpf.next()
                k.act(ei.t[:], pb.t[:], AF.Exp, [pb.r], [ei.r], scale=1.0 / 16.0)
                k.tt("dve", ke[d][:, tb * 512:(tb + 1) * 512], xk.t[:], ei.t[:], ALU.mult, [xk.r, ei.r], [rke[d][tb]])
                ee = self.tmpf.next()
                k.act(ee.t[:], pe_.t[:], AF.Exp, [pe_.r], [ee.r], scale=-1.0 / 16.0)
                k.tt("dve", kend[d][:, tb * 4:(tb + 1) * 4, :], xktv, ee.t[:, :].rearrange("p (c f) -> p c f", f=128), ALU.mult, [xkt.r, ee.r], [rkend[d][tb]])

        def state_step(S, d, c, store):
            pu = self.ps.next()
            k.mm(pu.t[:, 0:256], kend[d][:, c, :], V[:, c, :], True, True, [rkend[d][c // 4], rV[c // 4]], [pu.r])
            store()
            for h in range(2):
                hs = slice(h * 64, (h + 1) * 64)
                k.stt("dve", S.t[hs, :], S.t[hs, :], dec.t[hs, d, c:c + 1], pu.t[hs, h * 128:(h + 1) * 128], ALU.mult, ALU.add,
                      [S.r, rdec[d][c // 4], pu.r], [S.r])

        k.memset("pool", Sbk.t[:], 0.0, [Sbk.r])
        k.memset("pool", Sf.t[:], 0.0, [Sf.r])
        for c in range(31, -1, -1):
            state_step(Sbk, 1, c, lambda c=c: k.copy("act", Sb[:, c, :], Sbk.t[:], [Sbk.r], [rSb[c]]))
        for tb in range(8):
            po = [self.acc[0], self.acc[1]]
            grt = []
            for h in range(2):
                g = self.tmpf.next()
                k.dma(g.t[:], self.gr[h * 128:(h + 1) * 128, tb * 512:(tb + 1) * 512], [], [g.r], g.name)
                grt.append(g)
            for cc in range(4):
                c = tb * 4 + cc
                cs_ = slice(c * 128, (c + 1) * 128)
                sfb = Sfb.next()
                state_step(Sf, 0, c, lambda sfb=sfb: k.copy("act", sfb.t[:], Sf.t[:], [Sf.r], [sfb.r]))
                for h in range(2):
                    hs = slice(h * 64, (h + 1) * 64)
                    ats = []
                    for d in range(2):
                        pa = self.ps.next()
                        k.mm(pa.t[:, 0:128], ke[d][hs, cs_], qe[d][hs, cs_], True, True, [rke[d][tb], rqe[d][tb]], [pa.r])
                        at = self.tmpb.next()
                        k.tt("dve", at.t[:, 0:128], pa.t[:, 0:128], TR[:, 4 + d, :], ALU.mult, [pa.r, self.trit.r], [at.r])
                        ats.append(at)
                    o = po[h].t[:, cc * 128:(cc + 1) * 128]
                    Vh = V[:, c, h * 128:(h + 1) * 128]
                    k.mm(o, Vh, ats[0].t[:, 0:128], True, False, [rV[tb], ats[0].r], [po[h].r])
                    k.mm(o, Vh, ats[1].t[:, 0:128], False, False, [rV[tb], ats[1].r], [po[h].r])
                    k.mm(o, sfb.t[hs, :], qe[0][hs, cs_], False, False, [sfb.r, rqe[0][tb]], [po[h].r])
                    k.mm(o, Sb[hs, c, :], qe[1][hs, cs_], False, True, [rSb[c], rqe[1][tb]], [po[h].r])
            for h in range(2):
                sq = self.tmpb.next()
                k.act(sq.t[:], po[h].t[:], AF.Square, [po[h].r], [sq.r])
                pn = self.ps.next()
                k.mm(pn.t[:], self.ones.t[:], sq.t[:], True, True, [sq.r, self.ones.r], [pn.r])
                rs = self.tmpf.next()
                self.rstd(rs.t[:], pn.t[:], 1.0 / 128.0, 0, [pn.r], rs.r)
                y = self.tmpf.next()
                k.stt("dve", y.t[:], po[h].t[:], self.pv.t[:, 2:3], rs.t[:], ALU.mult, ALU.mult, [po[h].r, rs.r, self.pv.r], [y.r])
                sl = self.tmpf.next()
                k.act(sl.t[:], grt[h].t[:], AF.Silu, [grt[h].r], [sl.r])
                yb = self.tmpb.next()
                k.tt("dve", yb.t[:], y.t[:], sl.t[:], ALU.mult, [y.r, sl.r], [yb.r])
                k.dma(self.yT[self.ybase[1] + h * 128: self.ybase[1] + (h + 1) * 128, tb * 512:(tb + 1) * 512], yb.t[:], [yb.r], [], yb.name + "o", is_out=True)

    def mla(self):
        k = self.k
        AB, AFt = self.AB.t, self.AF.t
        cqn = lambda kc, c0, n: AB[:, kc * 4096 + c0: kc * 4096 + c0 + n]
        ckvn = lambda kc, c0, n: AB[:, 8192 + kc * 4096 + c0: 8192 + kc * 4096 + c0 + n]
        QN = AB[:, 16384:20480]; QR = AB[0:64, 20480:24576]; KN = AB[:, 24576:28672]; KRo = AB[0:64, 28672:32768]
        V = AB[:, 32768:36864].rearrange("p (c f) -> p c f", f=128)
        KR = AFt[0:64, 0:4096]
        rcq = [Res("cqn") for _ in range(8)]
        rckv = [Res("ckvn") for _ in range(8)]
        rKR = [Res("KR") for _ in range(8)]
        wq, wkv, sacc, onesf = self.mla_small()
        k.memset("pool", onesf.t[:], 1.0, writes=[onesf.r])
        for (dst, src) in ((wq, self.wuq), (wkv, self.wukv)):
            for h in range(2):
                st_ = self.tmpf.next()
                sv = st_.t[:, :].rearrange("p (a f) -> p a f", f=256)
                k.dma(sv, src[h, :, :].rearrange("(a p) f -> p a f", p=128), [], [st_.r], st_.name)
                k.copy("dve", dst.t[:, h, :, :], sv, [st_.r], [dst.r])
        for (src, dst, res, gcol) in ((self.cq, cqn, rcq, 3), (self.ckv, ckvn, rckv, 5)):
            for tb in range(8):
                xs = []
                pb = self.ps.next()
                for kc in range(2):
                    x = self.tmpf.next()
                    k.dma(x.t[:], src[kc * 128:(kc + 1) * 128, tb * 512:(tb + 1) * 512], [], [x.r], x.name)
                    sq = self.tmpb.next()
                    k.act(sq.t[:], x.t[:], AF.Square, [x.r], [sq.r])
                    k.mm(pb.t[:], self.ones.t[:], sq.t[:], kc == 0, kc == 1, [sq.r, self.ones.r], [pb.r])
                    xs.append(x)
                rs = self.tmpf.next()
                self.rstd(rs.t[:], pb.t[:], 1.0 / 256.0, 0, [pb.r], rs.r)
                for kc in range(2):
                    k.stt("dve", dst(kc, tb * 512, 512), xs[kc].t[:], self.pv.t[:, gcol + kc: gcol + kc + 1], rs.t[:], ALU.mult, ALU.mult,
                          [xs[kc].r, rs.r, self.pv.r], [res[tb]])

        def rope_mix(out, a, b, ga, gb, rs_ap, cs_t, reads, wres):
            t1 = self.tmpf.next()
            t2 = self.tmpf.next()
            if rs_ap is None:
                k.ts("dve", t1.t[0:64, :], a, ga, None, ALU.mult, None, reads, [t1.r])
                k.ts("dve", t2.t[0:64, :], b, gb, None, ALU.mult, None, reads, [t2.r])
            else:
                k.stt("dve", t1.t[0:64, :], a, ga, rs_ap, ALU.mult, ALU.mult, reads, [t1.r])
                k.stt("dve", t2.t[0:64, :], b, gb, rs_ap, ALU.mult, ALU.mult, reads, [t2.r])
            k.tt("dve", t1.t[0:64, :], t1.t[0:64, :], cs_t[0], ALU.mult, [t1.r, cs_t[2]], [t1.r])
            k.tt("pool", t2.t[0:64, :], t2.t[0:64, :], cs_t[1], ALU.mult, [t2.r, cs_t[2]], [t2.r])
            k.tt("dve", out, t1.t[0:64, :], t2.t[0:64, :], ALU.add, [t1.r, t2.r], [wres])

        def load_cs(tb):
            c1 = self.tmpf.next()
            k.dma(c1.t[0:64, :], self.cs[0:64, tb * 512:(tb + 1) * 512], [], [c1.r], c1.name)
            c2 = self.tmpf.next()
            k.dma(c2.t[0:64, :], self.cs[64:128, tb * 512:(tb + 1) * 512], [], [c2.r], c2.name)
            r = Res("cs")
            return (c1.t[0:64, :], c2.t[0:64, :], c1.r, c2.r)

        for tb in range(8):
            a = self.tmpf.next()
            self.load_kro(a, tb, False)
            b = self.tmpf.next()
            self.load_kro(b, tb, True)
            c1, c2, r1, r2 = load_cs(tb)
            t1 = self.tmpf.next()
            k.stt("dve", t1.t[0:64, :], a.t[0:64, :], self.pv.t[0:64, 11:12], c1, ALU.mult, ALU.mult, [a.r, r1, self.pv.r], [t1.r])
            t2 = self.tmpf.next()
            k.stt("dve", t2.t[0:64, :], b.t[0:64, :], self.pv.t[0:64, 12:13], c2, ALU.mult, ALU.mult, [b.r, r2, self.pv.r], [t2.r])
            k.tt("dve", KR[:, tb * 512:(tb + 1) * 512], t1.t[0:64, :], t2.t[0:64, :], ALU.add, [t1.r, t2.r], [rKR[tb]])
        for h in range(2):
            rQN = [Res("QN") for _ in range(8)]; rQR = [Res("QR") for _ in range(8)]
            rKN = [Res("KN") for _ in range(8)]; rKRo = [Res("KRo") for _ in range(8)]; rVv = [Res("V") for _ in range(8)]
            for tb in range(8):
                ts_ = slice(tb * 512, (tb + 1) * 512)
                pqn = self.ps.next(); pqr = self.ps.next(); pqs = self.ps.next()
                for (pp, m, c0) in ((pqn, 128, 0), (pqr, 64, 128), (pqs, 64, 192)):
                    for kc in range(2):
                        k.mm(pp.t[0:m, :], wq.t[:, h, kc, c0:c0 + m], cqn(kc, tb * 512, 512), kc == 0, kc == 1, [wq.r, rcq[tb]], [pp.r])
                s1 = self.tmpb.next()
                k.act(s1.t[:], pqn.t[:], AF.Square, [pqn.r], [s1.r])
                s2 = self.tmpb.next()
                k.act(s2.t[0:64, :], pqr.t[0:64, :], AF.Square, [pqr.r], [s2.r])
                pb = self.ps.next()
                k.mm(pb.t[:], self.ones.t[:], s1.t[:], True, False, [s1.r, self.ones.r], [pb.r])
                k.mm(pb.t[:], self.ones.t[0:64, :], s2.t[0:64, :], False, True, [s2.r, self.ones.r], [pb.r])
                rs = self.tmpf.next()
                self.rstd(rs.t[:], pb.t[:], 1.0, 2, [pb.r], rs.r)
                k.stt("dve", QN[:, ts_], pqn.t[:], self.pv.t[:, 7:8], rs.t[:], ALU.mult, ALU.mult, [pqn.r, rs.r, self.pv.r], [rQN[tb]])
                c1, c2, r1, r2 = load_cs(tb)
                t1 = self.tmpf.next()
                k.stt("dve", t1.t[0:64, :], pqr.t[0:64, :], self.pv.t[0:64, 8:9], rs.t[0:64, :], ALU.mult, ALU.mult, [pqr.r, rs.r, self.pv.r], [t1.r])
                k.tt("dve", t1.t[0:64, :], t1.t[0:64, :], c1, ALU.mult, [t1.r, r1], [t1.r])
                t2 = self.tmpf.next()
                k.stt("dve", t2.t[0:64, :], pqs.t[0:64, :], self.pv.t[0:64, 9:10], rs.t[0:64, :], ALU.mult, ALU.mult, [pqs.r, rs.r, self.pv.r], [t2.r])
                k.tt("pool", t2.t[0:64, :], t2.t[0:64, :], c2, ALU.mult, [t2.r, r2], [t2.r])
                k.tt("dve", QR[:, ts_], t1.t[0:64, :], t2.t[0:64, :], ALU.add, [t1.r, t2.r], [rQR[tb]])
                pkn = self.ps.next()
                for kc in range(2):
                    k.mm(pkn.t[:], wkv.t[:, h, kc, 0:128], ckvn(kc, tb * 512, 512), kc == 0, kc == 1, [wkv.r, rckv[tb]], [pkn.r])
                kr = self.tmpf.next()
                self.load_kro(kr, tb, False)
                s3 = self.tmpb.next()
                k.act(s3.t[:], pkn.t[:], AF.Square, [pkn.r], [s3.r])
                s4 = self.tmpb.next()
                k.act(s4.t[0:64, :], kr.t[0:64, :], AF.Square, [kr.r], [s4.r])
                pb2 = self.ps.next()
                k.mm(pb2.t[:], self.ones.t[:], s3.t[:], True, False, [s3.r, self.ones.r], [pb2.r])
                k.mm(pb2.t[:], self.ones.t[0:64, :], s4.t[0:64, :], False, True, [s4.r, self.ones.r], [pb2.r])
                rk_ = self.tmpf.next()
                self.rstd(rk_.t[:], pb2.t[:], 1.0 / 192.0, 0, [pb2.r], rk_.r)
                k.stt("dve", KN[:, ts_], pkn.t[:], self.pv.t[:, 10:11], rk_.t[:], ALU.mult, ALU.mult, [pkn.r, rk_.r, self.pv.r], [rKN[tb]])
                k.tt("dve", KRo[:, ts_], KR[:, ts_], rk_.t[0:64, :], ALU.mult, [rKR[tb], rk_.r], [rKRo[tb]])
                pvv = self.ps.next()
                for c in range(4):
                    for kc in range(2):
                        k.mm(pvv.t[:, c * 128:(c + 1) * 128], ckvn(kc, tb * 512 + c * 128, 128), wkv.t[:, h, kc, 128:256], kc == 0, kc == 1,
                             [wkv.r, rckv[tb]], [pvv.r])
                k.copy("act", V[:, tb * 4:(tb + 1) * 4, :], pvv.t[:, :].rearrange("p (c f) -> p c f", f=128), [pvv.r], [rVv[tb]])
            items = [(qb, kc) for qb in range(8) for kc in range(32)]

            def qk(i):
                qb, kc = items[i]
                qs = slice(qb * 512, (qb + 1) * 512)
                ks = slice(kc * 128, (kc + 1) * 128)
                ps = self.ps_att_next()
                k.mm(ps.t[:], KN[:, ks], QN[:, qs], True, False, [rKN[kc // 4], rQN[qb]], [ps.r])
                k.mm(ps.t[:], KRo[:, ks], QR[:, qs], False, True, [rKRo[kc // 4], rQR[qb]], [ps.r])
                return ps

            pending = qk(0)
            for i, (qb, kc) in enumerate(items):
                qs = slice(qb * 512, (qb + 1) * 512)
                ps = pending
                if i + 1 < len(items):
                    pending = qk(i + 1)
                po = self.acc[(qb % 2)]
                if kc == 0:
                    sa = sacc.next()
                pT = self.tmpb.next()
                k.act(pT.t[:], ps.t[:], AF.Exp, [ps.r], [pT.r])
                k.mm(po.t[:], V[:, kc, :], pT.t[:], kc == 0, kc == 31, [rVv[kc // 4], pT.r], [po.r])
                if kc == 0:
                    k.copy("dve", sa.t[:], pT.t[:], [pT.r], [sa.r])
                else:
                    k.tt("dve", sa.t[:], sa.t[:], pT.t[:], ALU.add, [sa.r, pT.r], [sa.r])
                if kc == 31:
                    pl = self.acc[2]
                    k.mm(pl.t[:], onesf.t[:], sa.t[:], True, True, [onesf.r, sa.r], [pl.r])
                    rc = self.tmpf.next()
                    k.act(rc.t[:], pl.t[:], AF.Ln, reads=[pl.r], writes=[rc.r])
                    k.act(rc.t[:], rc.t[:], AF.Exp, reads=[rc.r], writes=[rc.r], scale=-1.0)
                    y = self.tmpb.next()
                    k.tt("dve", y.t[:], po.t[:], rc.t[:], ALU.mult, [po.r, rc.r], [y.r])
                    k.dma(self.yT[self.ybase[2] + h * 128: self.ybase[2] + (h + 1) * 128, qs], y.t[:], [y.r], [], y.name + "o", is_out=True)

    def ps_att_next(self):
        if not hasattr(self, "_pa"):
            self._pa = Rot(self.ps.b[0:4])
        return self._pa.next()


def _swap64(v):
    return np.concatenate([v[..., 32:64], v[..., 0:32]], axis=-1)


def _pad128(v):
    o = np.zeros(128, np.float32)
    o[: v.shape[0]] = v
    return o


def const_tables():
    s = np.arange(128)[:, None]
    t = np.arange(128)[None, :]
    tri = np.stack([s <= t, s > t, s >= t, s < t, s <= t, s > t], axis=1).astype(np.float32)
    half = 32
    inv = (10000.0 ** (-np.arange(half, dtype=np.float32) / half)).astype(np.float32)
    ang = np.arange(SEQ, dtype=np.float32)[:, None] * inv[None, :]
    cos, sin = np.cos(ang).astype(np.float32).T, np.sin(ang).astype(np.float32).T
    cs = np.concatenate([cos, cos, -sin, sin], axis=0)
    return np.ascontiguousarray(tri), np.ascontiguousarray(cs)


def na_bias_table(rpb_heads):
    out = np.full((4, 128, 5, 5, 128), -30000.0, np.float32)
    reps = {0: 2, 1: 0, 2: 1, 3: 30, 4: 31}
    pidx = np.arange(128)
    kcol = pidx % 64
    qi = np.arange(128)
    qcol = qi % 64
    c0 = np.clip(qcol - 8, 0, 48)
    col_ok = (kcol[:, None] >= c0[None, :]) & (kcol[:, None] < c0[None, :] + 16)
    dc = np.clip(kcol[:, None] - qcol[None, :] + 15, 0, 30)
    for cls, t in reps.items():
        a0, ns, c = na_pair_info(t)
        assert c == cls
        qrow = 2 * t + qi // 64
        r0q = np.clip(qrow - 4, 0, 56)
        for sl in range(ns):
            krow = 2 * (a0 + sl) + pidx // 64
            row_ok = (krow[:, None] >= r0q[None, :]) & (krow[:, None] <= r0q[None, :] + 7)
            dr = np.clip(krow[:, None] - qrow[None, :] + 7, 0, 14)
            ok = row_ok & col_ok
            g = rpb_heads[:, dr, dc]
            out[:, :, cls, sl, :] = np.where(ok[None], g, np.float32(-30000.0))
    return np.ascontiguousarray(out.reshape(4, 128, NA_TAB))


def prep_B(zT, ztok, P, l, p, tri, cs):
    f = np.ascontiguousarray
    ones = np.ones((16, SEQ), np.float32)
    z15 = np.zeros((15, 128), np.float32)
    gsl = slice(p * 128, (p + 1) * 128)
    qn, kn = P["mla_q_norm"][l], P["mla_k_norm"][l]
    pv = np.zeros((128, 16), np.float32)
    pv[:, 0] = np.tile(P["na_q_norm"][l], 2)
    pv[:, 1] = np.tile(P["na_k_norm"][l], 2)
    pv[:, 2] = P["gla_out_norm"][l]
    pv[:, 3:5] = P["mla_cq_norm"][l].reshape(2, 128).T
    pv[:, 5:7] = P["mla_ckv_norm"][l].reshape(2, 128).T
    pv[:, 7] = qn[:128]
    pv[:, 8] = _pad128(qn[128:])
    pv[:, 9] = _pad128(_swap64(qn[128:]))
    pv[:, 10] = kn[:128]
    pv[:, 11] = _pad128(kn[128:])
    pv[:, 12] = _pad128(_swap64(kn[128:]))
    wuq, wukv = P["mla_w_uq"][l], P["mla_w_ukv"][l]
    wq = np.stack([np.concatenate([wuq[:, h * 192: h * 192 + 192], _swap64(wuq[:, h * 192 + 128: h * 192 + 192])], axis=1)
                   for h in (2 * p, 2 * p + 1)])
    wkv = np.stack([wukv[:, h * 256:(h + 1) * 256] for h in (2 * p, 2 * p + 1)])
    params = {"bcat": na_bias_table(P["na_rpb"][l][4 * p: 4 * p + 4]),
              "upf": f(np.concatenate([P["gla_gf_up"][l][:, gsl], P["gla_gf_bias"][l][None, gsl], z15], 0)),
              "upb": f(np.concatenate([P["gla_gb_up"][l][:, gsl], P["gla_gb_bias"][l][None, gsl], z15], 0)),
              "wuq": f(wq), "wukv": f(wkv), "pvec": pv}
    if zT is None:
        return params
    kr = zT[3616:3680]
    return {
        "naq": f(zT[p * 256:(p + 1) * 256]), "nak": f(zT[512 + p * 256: 512 + (p + 1) * 256]), "nav": f(ztok[:, p * 256:(p + 1) * 256]),
        "bcat": na_bias_table(P["na_rpb"][l][4 * p: 4 * p + 4]),
        "gq": f(zT[1536 + p * 128: 1536 + (p + 1) * 128]), "gk": f(zT[1792 + p * 128: 1792 + (p + 1) * 128]),
        "gkt": f(ztok[:, 512 + p * 128: 512 + (p + 1) * 128]), "gvt": f(ztok[:, 768 + p * 256: 768 + (p + 1) * 256]),
        "gr": f(zT[2560 + p * 256: 2560 + (p + 1) * 256]),
        "gff": f(np.concatenate([zT[3072:3088], ones], 0)), "gfb": f(np.concatenate([zT[3088:3104], ones], 0)),
        "upf": f(np.concatenate([P["gla_gf_up"][l][:, gsl], P["gla_gf_bias"][l][None, gsl], z15], 0)),
        "upb": f(np.concatenate([P["gla_gb_up"][l][:, gsl], P["gla_gb_bias"][l][None, gsl], z15], 0)),
        "tri": tri, "cq": f(zT[3104:3360]), "ckv": f(zT[3360:3616]),
        "kro": f(np.concatenate([kr, kr[32:64], kr[0:32]], 0)), "cs": cs,
        "wuq": f(wq), "wukv": f(wkv), "pvec": pv,
    }


class VBuf:
    def __init__(self, t, name):
        self.t = t
        self.r = Res(name)
        self.name = name


class Carve:
    def __init__(self, big, nwords):
        self.big, self.n, self.off = big, nwords, 0

    def f32(self, n):
        a = self.big[:, self.off:self.off + n]
        self.off += n
        assert self.off <= self.n, (self.off, self.n)
        return a

    def bf16(self, n):
        assert n % 2 == 0
        return self.f32(n // 2).bitcast(BF16)


BIGW = 44032 + 2048


class FProg(AProg, BProg):
    def __init__(self, L=4):
        nc = bass.Bass("TRN2", target_bir_lowering=False)
        self.nc = nc
        self.L = L
        din = lambda n, s: nc.dram_tensor(n, s, F32, kind="ExternalInput").ap()
        scr = lambda n, s: nc.dram_tensor(n, s, F32, kind="Internal").ap()
        self.xin_d = din("xin", [1024, SEQ])
        self.W = {n: din(n, [L] + sh) for n, sh in (
            ("f1w1", [1024, DFF]), ("f1w3", [1024, DFF]), ("f1w2", [DFF, 1024]),
            ("f2w1", [1024, DFF]), ("f2w3", [1024, DFF]), ("f2w2", [DFF, 1024]),
            ("win", [1024, DIN]), ("wbr", [1536, 1024]), ("wout", [1024, 1024]))}
        self.pva_d = din("pva", [L + 1, 128, 24])
        self.bcat_d = din("bcat", [L, 2, 4, 128, NA_TAB])
        self.upf_d = din("upf", [L, 2, 32, 128]); self.upb_d = din("upb", [L, 2, 32, 128])
        self.wuq_d = din("wuq", [L, 2, 2, 256, 256]); self.wukv_d = din("wukv", [L, 2, 2, 256, 256])
        self.pvb_d = din("pvb", [L, 2, 128, 16])
        self.tri = din("tri", [128, 6, 128]); self.cs = din("cs", [128, SEQ]); self.ones16 = din("ones16", [16, SEQ])
        self.xout_d = nc.dram_tensor("xout", [1024, SEQ], F32, kind="ExternalOutput").ap()
        self.XS = scr("XS", [1024, SEQ]); self.ZT = scr("ZT", [DIN, SEQ]); self.ZTOK = scr("ZTOK", [SEQ, NZT]); self.YT = nc.dram_tensor("YT", [1536, SEQ], BF16, kind="Internal").ap()
        with contextlib.ExitStack() as st:
            self.k = K(nc, st)
            self.build()
            self.k.finish()

    def load_gfl(self, g, d, tb):
        cols = slice(tb * 512, (tb + 1) * 512)
        self.k.dma(g.t[0:16, :], self.ZT[3072 + d * 16: 3088 + d * 16, cols], [], [g.r], g.name)
        self.k.dma(g.t[16:32, :], self.ones16[:, cols], [], [g.r], g.name)

    def load_kro(self, buf, tb, swapped):
        cols = slice(tb * 512, (tb + 1) * 512)
        if swapped:
            self.k.dma(buf.t[0:32, :], self.ZT[3648:3680, cols], [], [buf.r], buf.name)
            self.k.dma(buf.t[32:64, :], self.ZT[3616:3648, cols], [], [buf.r], buf.name)
        else:
            self.k.dma(buf.t[0:64, :], self.ZT[3616:3680, cols], [], [buf.r], buf.name)

    def gla_small(self):
        return self._gs

    def na_pt(self):
        return self._npt

    def mla_small(self):
        return self._ms

    def build(self):
        k = self.k
        L = self.L
        big = k.sb("BIG", [128, BIGW], F32).t
        self.tmpf = k.rot("tmpf", 8, [128, 512], F32)
        self.tmpb = k.rot("tmpb", 4, [128, 512], BF16)
        self.sq = self.tmpb
        self.ones = k.sb("ones", [128, 128], BF16)
        self.bd = k.sb("bd", [128, 128], BF16)
        self.cst = k.sb("cst", [128, 8], F32)
        self.epsb = self.cst
        self.trit = k.sb("trit", [128, 6, 128], F32)
        pvA = k.sb("pvA", [128, 24], F32)
        pvB = k.sb("pvB", [128, 16], F32)
        banks = k.psum(8)
        k.memset("pool", self.ones.t[:], 1.0, writes=[self.ones.r])
        k.memset("pool", self.bd.t[:], 0.0, writes=[self.bd.r])
        k.memset("pool", self.bd.t[0:64, 0:64], 1.0, writes=[self.bd.r])
        k.memset("pool", self.bd.t[64:128, 64:128], 1.0, writes=[self.bd.r])
        for j, v in enumerate((EPS, 64.0 * EPS, 192.0 * EPS, 1.0)):
            k.memset("pool", self.cst.t[:, j:j + 1], v, writes=[self.cst.r])
        k.dma(self.trit.t[:], self.tri[:, :, :], [], [self.trit.r], "tri")
        ca = Carve(big, BIGW)
        self.X = VBuf(ca.f32(8 * TOK).rearrange("p (a b) -> p a b", a=8), "X")
        self.arena = VBuf(ca.bf16(30720), "arena")
        self.stage = Rot([VBuf(ca.f32(WSLOT), "stg%d" % i) for i in range(2)])
        self.wb = Rot([VBuf(ca.bf16(WSLOT), "wb%d" % i) for i in range(4)])
        self.macc = Rot([VBuf(ca.f32(512), "macc%d" % i) for i in range(2)])
        self.gpool = Rot([VBuf(ca.f32(512), "gp%d" % i) for i in range(4)])
        cb = Carve(big, BIGW)
        self.AF = VBuf(cb.f32(13952), "AF")
        self.AB = VBuf(cb.bf16(36864), "AB")
        self._gs = (VBuf(cb.f32(64).rearrange("p (a b) -> p a b", a=2), "dec"),
                    VBuf(cb.f32(256)[0:32, :].rearrange("p (a b) -> p a b", a=2), "gup"),
                    VBuf(cb.bf16(256)[0:32, :].rearrange("p (a b) -> p a b", a=2), "gupb"),
                    VBuf(cb.f32(128), "Sf"), VBuf(cb.f32(128), "Sbk"),
                    Rot([VBuf(cb.bf16(128), "Sfb%d" % i) for i in range(2)]))
        self._ms = (VBuf(cb.bf16(1024).rearrange("p (a b c) -> p a b c", a=2, b=2), "wq"),
                    VBuf(cb.bf16(1024).rearrange("p (a b c) -> p a b c", a=2, b=2), "wkv"),
                    Rot([VBuf(cb.f32(512), "sacc%d" % i) for i in range(2)]), VBuf(cb.f32(128), "onesf"))
        self.banks = banks.b
        self._npt = Rot([VBuf(cb.bf16(640), "napt%d" % i) for i in range(3)])
        psA = Rot(banks.b)
        psB = Rot(banks.b[0:5])
        self._pa = Rot(banks.b[0:4])
        self.acc = banks.b[5:8]
        self.yT = self.YT
        for l in range(L + 1):
            do_k3, do_k1 = l > 0, l < L
            self.ps = psA
            self.pv = pvA
            k.dma(pvA.t[:], self.pva_d[l, :, :], [], [pvA.r], "pvA")
            for half in range(2):
                hs = slice(half * TOK, (half + 1) * TOK)
                self.Xr = [[Res("X") for _ in range(NTB)] for _ in range(8)]
                src = self.xin_d if l == 0 else self.XS
                for dc in range(8):
                    k.dma(self.X.t[:, dc, :], src[dc * 128:(dc + 1) * 128, hs], reads=[], writes=self.Xr[dc], key="xin%d" % dc)
                if do_k3:
                    self.gates = self.ZT[3680:6752, hs]
                    self.yin = self.YT[:, hs]
                    self.wbr = self.W["wbr"][l - 1]
                    self.wout = self.W["wout"][l - 1]
                    self.k3()
                    k.S.barrier()
                    self.ffn(0, self.W["f2w1"][l - 1], self.W["f2w3"][l - 1], self.W["f2w2"][l - 1])
                    k.S.barrier()
                if do_k1:
                    self.ffn(8, self.W["f1w1"][l], self.W["f1w3"][l], self.W["f1w2"][l])
                    k.S.barrier()
                    self.win = self.W["win"][l]
                    self.zT = self.ZT[:, hs]
                    self.ztok = self.ZTOK[hs, :]
                    self.zphase()
                dst = self.XS if do_k1 else self.xout_d
                for dc in range(8):
                    k.dma(dst[dc * 128:(dc + 1) * 128, hs], self.X.t[:, dc, :], reads=self.Xr[dc], writes=[], key="xo%d" % dc, is_out=True)
                k.S.barrier()
            if not do_k1:
                break
            self.ps = psB
            self.pv = pvB
            self.yT = self.YT
            for p in range(2):
                k.dma(pvB.t[:], self.pvb_d[l, p, :, :], [], [pvB.r], "pvB")
                self.ybase = (p * 256, 512 + p * 256, 1024 + p * 256)
                Z, ZK = self.ZT, self.ZTOK
                self.naq = Z[p * 256:(p + 1) * 256, :]; self.nak = Z[512 + p * 256: 512 + (p + 1) * 256, :]
                self.nav = ZK[:, p * 256:(p + 1) * 256]
                self.bcat = self.bcat_d[l, p]
                self.gq = Z[1536 + p * 128: 1536 + (p + 1) * 128, :]; self.gk = Z[1792 + p * 128: 1792 + (p + 1) * 128, :]
                self.gkt = ZK[:, 512 + p * 128: 512 + (p + 1) * 128]; self.gvt = ZK[:, 768 + p * 256: 768 + (p + 1) * 256]
                self.gr = Z[2560 + p * 256: 2560 + (p + 1) * 256, :]
                self.upf = self.upf_d[l, p]; self.upb = self.upb_d[l, p]
                self.cq = Z[3104:3360, :]; self.ckv = Z[3360:3616, :]
                self.wuq = self.wuq_d[l, p]; self.wukv = self.wukv_d[l, p]
                self.na()
                k.S.barrier()
                self.gla()
                k.S.barrier()
                self.mla()
                k.S.barrier()


def _gain_cols(g):
    return np.ascontiguousarray(g.reshape(8, 128).T)


_PROG = {}


def kernel(**inputs):
    P = {k_: np.asarray(v, dtype=np.float32) for k_, v in inputs.items()}
    x = P["x"]
    Bn, S, D = x.shape
    L = P["w_in"].shape[0]
    tri, cs = const_tables()
    pva = np.zeros((L + 1, 128, 24), np.float32)
    for l in range(L + 1):
        if l > 0:
            pva[l, :, 0:8] = _gain_cols(P["ffn2_norm"][l - 1])
        if l < L:
            pva[l, :, 8:16] = _gain_cols(P["ffn1_norm"][l])
            pva[l, :, 16:24] = _gain_cols(P["mix_norm"][l])
    bp = [[prep_B(None, None, P, l, p, tri, cs) for p in range(2)] for l in range(L)]
    stack = lambda name: np.ascontiguousarray(np.stack([np.stack([bp[l][p][name] for p in range(2)]) for l in range(L)]))
    common = {
        "f1w1": P["ffn1_w1"], "f1w3": P["ffn1_w3"], "f1w2": P["ffn1_w2"],
        "f2w1": P["ffn2_w1"], "f2w3": P["ffn2_w3"], "f2w2": P["ffn2_w2"],
        "win": P["w_in"], "wout": P["w_out"],
        "wbr": np.ascontiguousarray(np.concatenate([P["w_br_na"], P["w_br_gla"], P["w_br_mla"]], axis=1)),
        "pva": pva, "bcat": stack("bcat"), "upf": stack("upf"), "upb": stack("upb"),
        "wuq": stack("wuq"), "wukv": stack("wukv"), "pvb": stack("pvec"),
        "tri": tri, "cs": cs, "ones16": np.ones((16, SEQ), np.float32),
    }
    if L not in _PROG:
        _PROG[L] = FProg(L)
    prog = _PROG[L]
    in_maps = []
    for b in range(Bn):
        m = dict(common)
        m["xin"] = np.ascontiguousarray(x[b].T)
        in_maps.append(m)
    res = run_bass_kernel_spmd(prog.nc, in_maps, core_ids=list(range(Bn))).results
    out = np.stack([np.asarray(res[b]["xout"]).T for b in range(Bn)], axis=0)
    return np.ascontiguousarray(out.astype(np.float32))
```
